# Optimizing a Trainium2 kernel written in Bass

```python
import jax, jax.numpy as jnp
from jax import lax
import numpy as np

D_MODEL = 1024
BATCH = 8
SEQ = 2048
DEPTH = 2

N_A_LAYERS = DEPTH // 2
N_B_LAYERS = DEPTH - N_A_LAYERS
POOL_WINDOWS = (2, 4, 8, 16)
N_POOL_GROUPS = len(POOL_WINDOWS)
POOL_GROUP = D_MODEL // N_POOL_GROUPS
HEAD_DIM = 64
N_HEADS = D_MODEL // HEAD_DIM
N_KV_HEADS = 4
GQA_GROUP = N_HEADS // N_KV_HEADS
WINDOW = 128
BLOCK = 128
D_FF = -(-8 * D_MODEL // (3 * 256)) * 256
PLE_DIM = 256
EPS = 1e-6
NEG_INF = -1e30

kernel_name = "yoco_pool_swa_sink_hybrid"


def rmsnorm(x, g):
    xf = x.astype(jnp.float32)
    y = xf * lax.rsqrt(jnp.mean(xf * xf, axis=-1, keepdims=True) + EPS)
    return (y * g.astype(jnp.float32)).astype(x.dtype)


def causal_multiscale_pool(h):
    B, S, C = h.shape
    hf = h.astype(jnp.float32)
    cp = jnp.concatenate([jnp.zeros((B, 1, C), jnp.float32), jnp.cumsum(hf, axis=1)], axis=1)
    pos1 = jnp.arange(1, S + 1, dtype=jnp.int32)
    outs = []
    for gi, w in enumerate(POOL_WINDOWS):
        sl = cp[:, :, gi * POOL_GROUP:(gi + 1) * POOL_GROUP]
        hi = sl[:, 1:]
        lo = jnp.concatenate([jnp.zeros((B, w - 1, POOL_GROUP), jnp.float32),
                              sl[:, :S - w + 1]], axis=1)
        cnt = jnp.minimum(pos1, w).astype(jnp.float32)[None, :, None]
        outs.append((hi - lo) / cnt)
    pooled = jnp.concatenate(outs, axis=-1)
    return (pooled - hf).astype(h.dtype)


def pool_mixer(h, w_pool, scale):
    B, S, _ = h.shape
    d = causal_multiscale_pool(h).reshape(B, S, N_POOL_GROUPS, POOL_GROUP)
    y = jnp.einsum('bsgc,gcd->bsgd', d, w_pool).reshape(B, S, D_MODEL)
    return y * scale


def alibi_slopes():
    h = jnp.arange(1, N_HEADS + 1, dtype=jnp.float32)
    return jnp.exp2(-8.0 * h / N_HEADS).reshape(N_KV_HEADS, GQA_GROUP)


def swa_sink_attention(q, k, v, sinks):
    B, S = q.shape[:2]
    nb = S // BLOCK
    qb = q.reshape(B, nb, BLOCK, N_KV_HEADS, GQA_GROUP, HEAD_DIM)
    kb = k.reshape(B, nb, BLOCK, N_KV_HEADS, HEAD_DIM)
    vb = v.reshape(B, nb, BLOCK, N_KV_HEADS, HEAD_DIM)
    pad = ((0, 0), (1, 0), (0, 0), (0, 0), (0, 0))
    kwin = jnp.concatenate([jnp.pad(kb, pad)[:, :-1], kb], axis=2)
    vwin = jnp.concatenate([jnp.pad(vb, pad)[:, :-1], vb], axis=2)
    scores = jnp.einsum('bnqkgd,bnskd->bnkgqs', qb, kwin,
                        preferred_element_type=jnp.float32) * (HEAD_DIM ** -0.5)
    qi = jnp.arange(BLOCK)[:, None]
    si = jnp.arange(2 * BLOCK)[None, :]
    rel = BLOCK + qi - si
    kpos = (jnp.arange(nb)[:, None, None] - 1) * BLOCK + si[None]
    valid = (rel >= 0)[None] & (rel < WINDOW)[None] & (kpos >= 0)
    bias = -alibi_slopes()[:, :, None, None] * rel.astype(jnp.float32)
    scores = jnp.where(valid[None, :, None, None], scores + bias[None, None], NEG_INF)
    sink = jnp.broadcast_to(sinks.astype(jnp.float32).reshape(N_KV_HEADS, GQA_GROUP)[None, None, :, :, None, None],
                            scores.shape[:-1] + (1,))
    probs = jax.nn.softmax(jnp.concatenate([scores, sink], axis=-1), axis=-1)[..., :-1]
    out = jnp.einsum('bnkgqs,bnskd->bnqkgd', probs.astype(v.dtype), vwin)
    return out.reshape(B, S, N_HEADS * HEAD_DIM)


def swiglu(h, w_gu, w_down):
    gu = h @ w_gu
    g, u = gu[..., :D_FF], gu[..., D_FF:]
    return (jax.nn.silu(g) * u) @ w_down


def setup_inputs(seed: int = 0) -> dict:
    key = jax.random.key(seed)
    ks = jax.random.split(key, 24)
    f32 = jnp.float32
    nrm = lambda k, s, fan: jax.random.normal(k, s, f32) * (fan ** -0.5)
    gain = lambda k, s: 1.0 + 0.1 * jax.random.normal(k, s, f32)
    KV = N_KV_HEADS * HEAD_DIM
    QD = N_HEADS * HEAD_DIM
    return {
        "x": jax.random.normal(ks[0], (BATCH, SEQ, D_MODEL), f32),
        "p": jax.random.normal(ks[1], (DEPTH, BATCH, SEQ, PLE_DIM), f32),
        "pre_mix_g": gain(ks[2], (DEPTH, D_MODEL)),
        "post_mix_g": gain(ks[3], (DEPTH, D_MODEL)),
        "pre_ffn_g": gain(ks[4], (DEPTH, D_MODEL)),
        "post_ffn_g": gain(ks[5], (DEPTH, D_MODEL)),
        "pool_w": nrm(ks[6], (N_A_LAYERS, N_POOL_GROUPS, POOL_GROUP, POOL_GROUP), POOL_GROUP),
        "pool_scale": gain(ks[7], (N_A_LAYERS, D_MODEL)),
        "kv_g": gain(ks[8], (D_MODEL,)),
        "w_kv": nrm(ks[9], (D_MODEL, 2 * KV), D_MODEL),
        "w_q": nrm(ks[10], (N_B_LAYERS, D_MODEL, QD), D_MODEL),
        "sinks": 0.5 * jax.random.normal(ks[11], (N_B_LAYERS, N_HEADS), f32),
        "w_o": nrm(ks[12], (N_B_LAYERS, QD, D_MODEL), QD),
        "w_gu": nrm(ks[13], (DEPTH, D_MODEL, 2 * D_FF), D_MODEL),
        "w_down": nrm(ks[14], (DEPTH, D_FF, D_MODEL), D_FF),
        "ple_g": gain(ks[15], (DEPTH, D_MODEL)),
        "w_ple_gate": nrm(ks[16], (DEPTH, D_MODEL, D_MODEL), D_MODEL),
        "w_ple_proj": nrm(ks[17], (DEPTH, PLE_DIM, D_MODEL), PLE_DIM),
        "ple_post_g": gain(ks[18], (DEPTH, D_MODEL)),
    }


def reference(x, p, pre_mix_g, post_mix_g, pre_ffn_g, post_ffn_g, pool_w, pool_scale,
              kv_g, w_kv, w_q, sinks, w_o, w_gu, w_down, ple_g, w_ple_gate, w_ple_proj,
              ple_post_g):
    B, S, _ = x.shape
    KV = N_KV_HEADS * HEAD_DIM
    k = v = None
    for i in range(DEPTH):
        h = rmsnorm(x, pre_mix_g[i])
        if i < N_A_LAYERS:
            y = pool_mixer(h, pool_w[i], pool_scale[i])
        else:
            b = i - N_A_LAYERS
            q = (h @ w_q[b]).reshape(B, S, N_HEADS, HEAD_DIM)
            y = swa_sink_attention(q, k, v, sinks[b]) @ w_o[b]
        x = x + rmsnorm(y, post_mix_g[i])
        h = rmsnorm(x, pre_ffn_g[i])
        x = x + rmsnorm(swiglu(h, w_gu[i], w_down[i]), post_ffn_g[i])
        gate = jax.nn.sigmoid(rmsnorm(x, ple_g[i]) @ w_ple_gate[i])
        e = (p[i].astype(x.dtype) @ w_ple_proj[i]) * gate
        x = x + rmsnorm(e, ple_post_g[i])
        if i == N_A_LAYERS - 1:
            kv = rmsnorm(x, kv_g) @ w_kv
            k = kv[..., :KV].reshape(B, S, N_KV_HEADS, HEAD_DIM)
            v = kv[..., KV:].reshape(B, S, N_KV_HEADS, HEAD_DIM)
    return x
```

```python
import numpy as np
from contextlib import ExitStack
import concourse.bass as bass
import concourse.mybir as mybir
from concourse.bass_utils import run_bass_kernel_spmd

F32 = mybir.dt.float32
BF16 = mybir.dt.bfloat16
AF = mybir.ActivationFunctionType
ALU = mybir.AluOpType

S = 2048
D = 1024
NC8 = 8
T = 512
NT = S // T
DFF = 2816
NJ = DFF // 128
EPS = 1e-6
SCALE = 0.125
ENGS = ("pe", "act", "dve", "pool", "sp")

GV = {}
for _l in range(2):
    for _k, _n in enumerate(["pre_mix", "post_mix", "pre_ffn", "post_ffn", "ple_g", "ple_post"]):
        GV[(_n, _l)] = _l * 6 + _k
GV["pool_scale"] = 12
GV["kv_g"] = 13
NV = 14


class Prog:
    def __init__(self, nc, dry, sig):
        self.nc = nc
        self.dry = dry
        self.sig = sig
        self.need = {e: set() for e in ENGS}
        self.n = {e: 0 for e in ENGS}
        self.cnt = {e: 0 for e in ENGS}
        self.val = {e: {} for e in ENGS}
        self.waited = {e: {} for e in ENGS}
        self.last_w = {}
        self.readers = {}
        self.retired = []
        self.bufrange = {}
        self.seen = set()
        self.sems = {}
        self.semtot = {}
        self.es = ExitStack()
        if not dry:
            self.h = {"pe": nc.tensor, "act": nc.scalar, "dve": nc.vector, "pool": nc.gpsimd, "sp": nc.sync}
            self.esem = {e: self.es.enter_context(nc.semaphore("s_" + e)) for e in ENGS}
        self.nwaits = 0

    def sem(self, name):
        if name not in self.sems:
            self.sems[name] = None if self.dry else self.es.enter_context(self.nc.semaphore("d_" + name))
            self.semtot[name] = 0
        return self.sems[name]

    def _deps(self, r, w):
        deps = {}
        rdeps = {}

        def add(dst, d):
            k = (d[0], d[1])
            o = dst.get(k)
            if o is None or o[2] < d[2]:
                dst[k] = d
        for k in list(r) + list(w):
            if k not in self.seen:
                self.seen.add(k)
                mine = self.bufrange.get(k[0])
                for rng, ds in self.retired:
                    hit = True
                    if rng is not None and mine is not None:
                        hit = any(a0 < b1 and b0 < a1 for (a0, a1) in rng for (b0, b1) in mine)
                    if hit:
                        for d in ds:
                            if self.dry:
                                rdeps[d] = d
                            else:
                                add(rdeps, d)
            d = self.last_w.get(k)
            if d is not None:
                add(deps, d)
        for k in w:
            for d in self.readers.get(k, ()):
                add(deps, d)
        return list(deps.values()) + list(rdeps.values())

    def _update(self, me, r, w):
        for k in w:
            self.last_w[k] = me
            self.readers[k] = []
        for k in r:
            if k in w:
                continue
            self.readers.setdefault(k, []).append(me)

    def _emit_waits(self, eng, deps):
        want = {}
        for d in deps:
            if d[0] == "c":
                _, pe_, idx = d
                if pe_ == eng and eng == "pe":
                    continue
                if self.dry:
                    self.need[pe_].add(idx)
                    continue
                v = self.val[pe_][idx]
                key = ("c", pe_)
                want[key] = max(want.get(key, 0), v)
            else:
                _, sname, v = d
                key = ("d", sname)
                want[key] = max(want.get(key, 0), v)
        if self.dry:
            return
        for key, v in want.items():
            if self.waited[eng].get(key, 0) >= v:
                continue
            self.waited[eng][key] = v
            s = self.esem[key[1]] if key[0] == "c" else self.sems[key[1]]
            self.h[eng].wait_ge(s, v)
            self.nwaits += 1

    def op(self, eng, fn, r=(), w=()):
        deps = self._deps(r, w)
        idx = self.n[eng]
        self.n[eng] += 1
        me = ("c", eng, idx)
        self._emit_waits(eng, deps)
        if not self.dry:
            ins = fn(self.h[eng])
            if idx in self.sig[eng]:
                self.cnt[eng] += 1
                ins.then_inc(self.esem[eng], 1)
                self.val[eng][idx] = self.cnt[eng]
        self._update(me, r, w)
        return me

    def dma(self, q, fn, semname, r=(), w=()):
        deps = self._deps(r, w)
        self.n[q] += 1
        self._emit_waits(q, deps)
        s = self.sem(semname)
        self.semtot[semname] += 16
        me = ("d", semname, self.semtot[semname])
        if not self.dry:
            o, i = fn()
            self.h[q].dma_start(out=o, in_=i).then_inc(s, 16)
        self._update(me, r, w)
        return me

    def retire(self, prefixes):
        per = {}
        for k in list(self.seen):
            if k[0] in prefixes:
                d = self.last_w.pop(k, None)
                ds = self.readers.pop(k, [])
                if d is not None:
                    ds = ds + [d]
                best = per.setdefault(k[0], {})
                for d in ds:
                    key = (d[0], d[1])
                    old = best.get(key)
                    if old is None or old[2] < d[2]:
                        best[key] = d
                self.seen.discard(k)
        for pfx, best in per.items():
            self.retired.append((self.bufrange.pop(pfx, None), list(best.values())))
        for pfx in prefixes:
            self.bufrange.pop(pfx, None)

    def finish(self):
        deps = [("d", n, v) for n, v in self.semtot.items() if v > 0]
        for e in ENGS:
            if e != "sp" and self.n[e] > 0:
                pass
        self._emit_waits("sp", deps)


class Scope:
    uid = 0

    def __init__(self, P, prefixes):
        self.P = P
        self.prefixes = prefixes
        self.es = ExitStack()

    def sb(self, name, shape, dt, kp=None):
        if self.P.dry:
            return None
        Scope.uid += 1
        t = self.es.enter_context(self.P.nc.sbuf_tensor("%s_%d" % (name, Scope.uid), shape, dt))
        if kp is not None:
            ml = self.P.nc.lookup_mloc(t)
            self.P.bufrange.setdefault(kp, []).append((int(ml.addr), int(ml.addr) + int(ml.dims[1])))
        return t

    def __enter__(self):
        return self

    def __exit__(self, *a):
        self.P.retire(self.prefixes)
        self.es.close()
        return False


def build(dry, sig, stop_after=None):
    nc = None if dry else bass.Bass("TRN2", target_bir_lowering=False)
    P = Prog(nc, dry, sig)

    def dram(name, shape, dt=F32, kind="ExternalInput"):
        if dry:
            return None
        return nc.dram_tensor(name, shape, dt, kind=kind).ap()

    xT = dram("xT", [128, NC8, S])
    pT = dram("pT", [2, 128, 2, S])
    gains = dram("gains", [128, NV, NC8])
    sinkb = dram("sinkb", [128, 16])
    maskT = dram("maskT", [128, 256])
    invc = dram("invc", [128, 4, 2, 16])
    wpool = dram("wpool", [128, 4, 2, 256])
    wgu = dram("wgu", [2, NJ, 128, 2, NC8, 128])
    wdn = dram("wdn", [2, NC8, 128, NJ, 128])
    wgate = dram("wgate", [2, 128, NC8, D])
    wproj = dram("wproj", [2, 128, 2, D])
    wq = dram("wq", [128, NC8, D])
    wo = dram("wo", [128, NC8, D])
    wk = dram("wk", [128, NC8, 4, 128])
    wv = dram("wv", [128, NC8, 256])
    outT = dram("outT", [128, NC8, S], kind="ExternalOutput")

    top = Scope(P, [])
    X = top.sb("X", [128, NC8, S], F32)
    G = top.sb("G", [128, NV, NC8], F32)
    ONES = top.sb("ONES", [128, 128], BF16)
    OP = top.sb("OP", [128, 192], BF16)
    EPST = top.sb("EPST", [128, 2], F32)
    NSINK = top.sb("NSINK", [128, 16], F32)
    MT = top.sb("MT", [128, 256], F32)
    INVC = top.sb("INVC", [128, 4, 2, 16], F32)
    HB = top.sb("HB", [128, NC8, 2 * T], BF16)
    NSQ = 3
    SQ = top.sb("SQ", [128, NSQ, T], BF16)
    RS = top.sb("RS", [128, T], F32)
    SQP = top.sb("SQP", [128, NSQ, T], BF16)
    RSP = top.sb("RSP", [128, T], F32)
    YS = top.sb("YS", [128, NC8, T], F32)
    TMP2 = [top.sb("TMP%d" % q_, [128, T], F32) for q_ in range(2)]
    PS = [None] * 8
    if not dry:
        for b in range(8):
            PS[b] = top.es.enter_context(nc.psum_tensor("ps%d" % b, [128, 512], F32))

    psrot = [0]

    def ps_next():
        b = psrot[0]
        psrot[0] = (b + 1) % 6
        return b

    strot = [0]

    def ps_stat():
        b = 6 + strot[0]
        strot[0] ^= 1
        return b

    P.dma("sp", lambda: (G[:], gains[:, :, :]), "cstG", w=[("G",)])
    P.dma("sp", lambda: (NSINK[:], sinkb[:, :]), "cstS", w=[("nsink",)])
    P.dma("sp", lambda: (MT[:], maskT[:, :]), "cstM", w=[("mt",)])
    P.dma("sp", lambda: (INVC[:], invc[:, :, :, :]), "cstI", w=[("invc",)])
    for i in range(NT):
        P.dma("sp", lambda i=i: (X[:, :, i * T:(i + 1) * T], xT[:, :, i * T:(i + 1) * T]), "x%d" % i,
              r=([("x", 0)] if i >= 1 else []), w=[("x", i)])
    P.op("dve", lambda e: e.memset(ONES[:], 1.0), w=[("ones",)])
    P.op("dve", lambda e: e.memset(OP[:], 0.0), w=[("op",)])
    P.op("dve", lambda e: e.memset(OP[:, 64:128], 1.0), w=[("op",)])
    P.op("dve", lambda e: e.memset(EPST[:, 0:1], EPS), w=[("eps",)])
    P.op("dve", lambda e: e.memset(EPST[:, 1:2], 1.0), w=[("eps",)])
    P.op("pool", lambda e: e.tensor_scalar(out=NSINK[:], in0=NSINK[:], scalar1=-1.0, scalar2=None, op0=ALU.mult),
         r=[("nsink",)], w=[("nsink",)])

    sqrot = {False: [0], True: [0]}
    statbank = {False: [None], True: [None]}

    statq = {False: [], True: []}
    statcfg = {False: [SQ, "sq", NSQ, 2], True: [SQP, "sqp", NSQ, 2]}

    def stat_add(c, sq_fn, rkeys, pre=False):
        sq, sk, nsq, lag = statcfg[pre]
        if c == 0:
            statbank[pre][0] = ps_stat()
            sqrot[pre][0] = 0
        b = statbank[pre][0]
        sl = sqrot[pre][0]
        sqrot[pre][0] = (sl + 1) % nsq
        P.op("act", lambda e: sq_fn(e, sq[:, sl, :]), r=rkeys, w=[(sk, sl)])
        statq[pre].append(lambda: P.op(
            "pe", lambda e: e.matmul(PS[b][:], ONES[:], sq[:, sl, :], start=(c == 0), stop=(c == NC8 - 1)),
            r=[("ones",), (sk, sl)], w=[("ps", b)]))
        while len(statq[pre]) > lag:
            statq[pre].pop(0)()

    def stats_to_rs(pre=False):
        rs, rk = (RSP, "rsp") if pre else (RS, "rs")
        while statq[pre]:
            statq[pre].pop(0)()
        b = statbank[pre][0]
        P.op("act", lambda e, b=b: e.activation(out=rs[:], in_=PS[b][:], func=AF.Ln, bias=EPST[:, 0:1], scale=1.0 / D),
             r=[("ps", b), ("eps",)], w=[(rk,)])
        P.op("act", lambda e: e.activation(out=rs[:], in_=rs[:], func=AF.Exp, scale=-0.5), r=[(rk,)], w=[(rk,)])

    def drain(*gens):
        gens = [g for g in gens if g is not None]
        alive = True
        while alive:
            alive = False
            for g in gens:
                try:
                    next(g)
                    alive = True
                except StopIteration:
                    pass

    def run_bg(gens, k=1):
        for g in gens:
            for _ in range(k):
                try:
                    next(g)
                except StopIteration:
                    break

    def prenorm_g(i, v, out_fn, out_keys):
        cs = slice(i * T, (i + 1) * T)
        for c in range(NC8):
            stat_add(c, lambda e, o, c=c: e.activation(out=o, in_=X[:, c, cs], func=AF.Square), [("x", i)], pre=True)
            if c % 2 == 1:
                yield
        stats_to_rs(pre=True)
        yield
        for c in range(NC8):
            P.op("dve", lambda e, c=c: e.scalar_tensor_tensor(out=out_fn(c), in0=X[:, c, cs], scalar=G[:, v, c:c + 1],
                                                               in1=RSP[:], op0=ALU.mult, op1=ALU.mult),
                 r=[("x", i), ("G",), ("rsp",)], w=[out_keys(c)])
            if c % 2 == 1:
                yield

    def prenorm(i, v, out_fn, out_keys):
        drain(prenorm_g(i, v, out_fn, out_keys))

    hoisted = set()

    def pre_hb_g(tag, i, v):
        if (tag, i) in hoisted:
            return None
        hoisted.add((tag, i))
        q = i % 2
        return prenorm_g(i, v, lambda c: HB[:, c, q * T:(q + 1) * T], lambda c: ("hb", q, c))

    def pre_hb(tag, i, v):
        drain(pre_hb_g(tag, i, v))

    def pipeline(n, pre, main, post, hook=None):
        for i in range(min(2, n)):
            pre(i)
        for i in range(n):
            main(i)
            if i + 2 < n:
                pre(i + 2)
            elif hook is not None:
                drain(hook(i + 2 - n))
            post(i)

    def postnorm_begin(i, v, ys=None, yk="ys"):
        ys = YS if ys is None else ys
        cs = slice(i * T, (i + 1) * T)
        while statq[False]:
            statq[False].pop(0)()
        b = statbank[False][0]

        def gen():
            P.op("act", lambda e: e.activation(out=RS[:], in_=PS[b][:], func=AF.Ln, bias=EPST[:, 0:1], scale=1.0 / D),
                 r=[("ps", b), ("eps",)], w=[("rs",)])
            P.op("act", lambda e: e.activation(out=RS[:], in_=RS[:], func=AF.Exp, scale=-0.5), r=[("rs",)], w=[("rs",)])
            yield
            for c in range(NC8):
                q = c % 2
                P.op("dve", lambda e, c=c, q=q: e.scalar_tensor_tensor(out=TMP2[q][:], in0=ys[:, c, :], scalar=G[:, v, c:c + 1],
                                                                        in1=RS[:], op0=ALU.mult, op1=ALU.mult),
                     r=[(yk, c), ("G",), ("rs",)], w=[("tmp", q)])
                P.op("dve", lambda e, c=c, q=q: e.tensor_tensor(out=X[:, c, cs], in0=X[:, c, cs], in1=TMP2[q][:], op=ALU.add),
                     r=[("tmp", q)], w=[("x", i)])
                yield
        return gen()

    def postnorm_residual(i, v):
        drain(postnorm_begin(i, v))

    def evac_y(b, c, scale_ap=None, ys=None, yk="ys"):
        ys = YS if ys is None else ys
        if scale_ap is None:
            P.op("act", lambda e: e.activation(out=ys[:, c, :], in_=PS[b][:], func=AF.Copy), r=[("ps", b)], w=[(yk, c)])
            stat_add(c, lambda e, o: e.activation(out=o, in_=PS[b][:], func=AF.Square), [("ps", b)])
        else:
            P.op("act", lambda e: e.activation(out=ys[:, c, :], in_=PS[b][:], func=AF.Copy, scale=scale_ap()),
                 r=[("ps", b), ("G",)], w=[(yk, c)])
            stat_add(c, lambda e, o: e.activation(out=o, in_=PS[b][:], func=AF.Square, scale=scale_ap()), [("ps", b), ("G",)])

    def run_tiles(pre_g, main_g, post_v, hook, YSB):
        drain(pre_g(0))
        pend = None
        for i in range(NT):
            ys, yk = (YS, "ys") if i % 2 == 1 else (YSB, "ysb")
            bgs = [pend] if pend is not None else []
            if i + 1 < NT:
                nxt = pre_g(i + 1)
            else:
                nxt = hook(0) if hook is not None else None
            if nxt is not None:
                bgs.append(nxt)
            for step, _ in enumerate(main_g(i, ys, yk)):
                run_bg(bgs, 2 if step % 4 == 0 else 1)
            drain(*bgs)
            pend = postnorm_begin(i, post_v, ys, yk)
        drain(pend, hook(1) if hook is not None else None)

    def store_out():
        for i in range(NT):
            P.dma("sp", lambda i=i: (outT[:, :, i * T:(i + 1) * T], X[:, :, i * T:(i + 1) * T]), "out%d" % i, r=[("x", i)])
        P.finish()
        top.es.close()
        P.es.close()
        return nc, P

    def mixer_pool(hook=None):
        wins = (2, 4, 8, 16)
        with Scope(P, ["hf", "wp", "wpa", "wpb", "a32", "s2", "s4", "s8", "s16", "e16", "t16", "ysb", "sq8", "sqp8", "s2b", "s4b"]) as sc:
            W = 16 + T
            HF = [sc.sb("HFb%d" % q, [128, NC8, W], BF16, "hf") for q in range(2)]
            WP = sc.sb("WP", [128, 4, 2, 256], BF16, "wp")
            WPA = sc.sb("WPA", [128, 4, 2, 256], BF16, "wpa")
            WPB = sc.sb("WPB", [128, 4, 2, 256], BF16, "wpb")
            S2B = sc.sb("S2B", [128, 4, W], BF16, "s2b")
            S4B = [sc.sb("S4B%d" % q, [128, 4, W], BF16, "s4b") for q in range(2)]
            A32 = sc.sb("A32", [128, 8, 32], F32, "a32")
            S2 = sc.sb("S2", [128, 8, 32], F32, "s2")
            S4 = sc.sb("S4", [128, 6, 32], F32, "s4")
            S8 = sc.sb("S8", [128, 4, 32], F32, "s8")
            S16 = sc.sb("S16", [128, 2, 32], F32, "s16")
            E16 = sc.sb("E16", [128, 8, 16], BF16, "e16")
            T16 = sc.sb("T16", [128, 2, 16], F32, "t16")
            YSB = sc.sb("YSBp", [128, NC8, T], F32, "ysb")
            SQ8 = sc.sb("SQ8p", [128, NC8, T], BF16, "sq8")
            SQP8 = sc.sb("SQP8p", [128, NC8, T], BF16, "sqp8")
            P.dma("pool", lambda: (WP[:], wpool[:, :, :, :]), "wp", w=[("wp",)])
            for g in range(4):
                w_ = wins[g]
                P.op("dve", lambda e, g=g, w_=w_: e.tensor_scalar(out=WPA[:, g], in0=WP[:, g], scalar1=1.0 / w_, scalar2=None, op0=ALU.mult),
                     r=[("wp",)], w=[("wpa", g)])
                P.op("dve", lambda e, g=g, w_=w_: e.tensor_scalar(out=WPB[:, g], in0=WP[:, g], scalar1=(1.0 / w_ - 1.0) if g < 2 else -1.0, scalar2=None, op0=ALU.mult),
                     r=[("wp",)], w=[("wpb", g)])
            pv = GV["pool_scale"]

            def pre_g(i):
                q = i % 2
                hf = HF[q]
                if i == 0:
                    P.op("dve", lambda e: e.memset(hf[:, :, 0:16], 0.0), w=[("hf", q)])
                else:
                    P.op("dve", lambda e: e.tensor_copy(out=hf[:, :, 0:16], in_=HF[1 - q][:, :, T:T + 16]),
                         r=[("hf", 1 - q)], w=[("hf", q)])
                yield from prenorm_g(i, GV[("pre_mix", 0)], lambda c: hf[:, c, 16:W], lambda c: ("hf", q))
                P.op("dve", lambda e: e.tensor_tensor(out=S2B[:, :, 1:W], in0=hf[:, 4:8, 1:W], in1=hf[:, 4:8, 0:W - 1], op=ALU.add),
                     r=[("hf", q)], w=[("s2b",)])
                P.op("dve", lambda e: e.tensor_tensor(out=S4B[q][:, :, 3:W], in0=S2B[:, :, 3:W], in1=S2B[:, :, 1:W - 2], op=ALU.add),
                     r=[("s2b",)], w=[("s4b", q)])
                yield
                if i == 0:
                    P.op("dve", lambda e: e.tensor_copy(out=A32[:], in_=hf[:, :, 0:32]), r=[("hf", q)], w=[("a32",)])
                    P.op("dve", lambda e: e.tensor_tensor(out=S2[:, :, 1:32], in0=A32[:, :, 1:32], in1=A32[:, :, 0:31], op=ALU.add),
                         r=[("a32",)], w=[("s2",)])
                    P.op("dve", lambda e: e.tensor_tensor(out=S4[:, :, 3:32], in0=S2[:, 2:8, 3:32], in1=S2[:, 2:8, 1:30], op=ALU.add),
                         r=[("s2",)], w=[("s4",)])
                    P.op("dve", lambda e: e.tensor_tensor(out=S8[:, :, 7:32], in0=S4[:, 2:6, 7:32], in1=S4[:, 2:6, 3:28], op=ALU.add),
                         r=[("s4",)], w=[("s8",)])
                    P.op("dve", lambda e: e.tensor_tensor(out=S16[:, :, 15:32], in0=S8[:, 2:4, 15:32], in1=S8[:, 2:4, 7:24], op=ALU.add),
                         r=[("s8",)], w=[("s16",)])
                    srcs = [(lambda: S2[:, 0:2, 16:32], ("s2",)), (lambda: S4[:, 0:2, 16:32], ("s4",)),
                            (lambda: S8[:, 0:2, 16:32], ("s8",)), (lambda: S16[:, 0:2, 16:32], ("s16",))]
                    for g in range(4):
                        sf, sk = srcs[g]
                        P.op("dve", lambda e, g=g, sf=sf: e.tensor_tensor(out=T16[:], in0=sf(), in1=INVC[:, g, :, :], op=ALU.mult),
                             r=[sk, ("invc",)], w=[("t16",)])
                        P.op("dve", lambda e, g=g: e.tensor_tensor(out=E16[:, 2 * g:2 * g + 2, :], in0=T16[:],
                                                                  in1=A32[:, 2 * g:2 * g + 2, 16:32], op=ALU.subtract),
                             r=[("t16",), ("a32",)], w=[("e16", g)])

            def main_g(i, ys, yk):
                q = i % 2
                hf = HF[q]
                for g in range(4):
                    w_ = wins[g]
                    for cc in range(2):
                        b = ps_next()
                        c = 2 * g + cc
                        terms = []
                        for kc in range(2):
                            if g < 2:
                                for k in range(w_):
                                    terms.append(((WPB if k == 0 else WPA), kc,
                                                  (lambda kc=kc, k=k: hf[:, 2 * g + kc, 16 - k:16 - k + T]),
                                                  [("wpb" if k == 0 else "wpa", g), ("hf", q)]))
                            else:
                                terms.append((WPB, kc, (lambda kc=kc: hf[:, 2 * g + kc, 16:16 + T]), [("wpb", g), ("hf", q)]))
                                for m4 in range(w_ // 4):
                                    terms.append((WPA, kc,
                                                  (lambda kc=kc, m4=m4: S4B[q][:, 2 * g + kc - 4, 16 - 4 * m4:16 - 4 * m4 + T]),
                                                  [("wpa", g), ("s4b", q)]))
                        nmm = len(terms)
                        m = 0
                        for wt, kc, rf, rk in terms:
                            P.op("pe", lambda e, g=g, cc=cc, kc=kc, b=b, wt=wt, rf=rf, m=m, nmm=nmm: e.matmul(
                                PS[b][:], wt[:, g, kc, cc * 128:(cc + 1) * 128], rf(), start=(m == 0), stop=(m == nmm - 1)),
                                r=rk, w=[("ps", b)])
                            m += 1
                        if i == 0:
                            for kc in range(2):
                                P.op("pe", lambda e, g=g, cc=cc, kc=kc, b=b, m=m, nmm=nmm: e.matmul(
                                    PS[b][:, 0:16], WP[:, g, kc, cc * 128:(cc + 1) * 128], E16[:, 2 * g + kc, :],
                                    start=(kc == 0), stop=(kc == 1)),
                                    r=[("wp",), ("e16", g)], w=[("ps", b)])
                                m += 1
                        evac_y(b, c, scale_ap=lambda c=c: G[:, pv, c:c + 1], ys=ys, yk=yk)
                        yield

            saved_cfg = {k: list(v) for k, v in statcfg.items()}
            statcfg[False] = [SQ8, "sq8", 8, 8]
            statcfg[True] = [SQP8, "sqp8", 8, 8]
            run_tiles(pre_g, main_g, GV[("post_mix", 0)], hook, YSB)
            statcfg[False] = saved_cfg[False]
            statcfg[True] = saved_cfg[True]

    def ffn(l, hook=None):
        NS = 4
        tag = ("ffn", l)
        with Scope(P, ["aa", "wg", "wd", "sg"]) as sc:
            AA = sc.sb("AA", [128, NJ, 2 * T], BF16, "aa")
            WG = [sc.sb("WG%d" % s, [128, 2, NC8, 128], BF16, "wg") for s in range(NS)]
            WD = [sc.sb("WD%d" % s, [128, NJ, 128], BF16, "wd") for s in range(3)]
            SG = [sc.sb("SG%d" % s, [128, T], F32, "sg") for s in range(2)]
            for st in range(2):
                def load_g(j):
                    s = j % NS
                    P.dma("pool", lambda: (WG[s][:], wgu[l, j, :, :, :, :]), "wg%d" % s, w=[("wg", s)])

                def load_d(k):
                    s = k % 3
                    P.dma("pool", lambda: (WD[s][:], wdn[l, k % NC8, :, :, :]), "wd%d" % s, w=[("wd", s)])

                for j in range(NS):
                    load_g(j)
                for sub in range(2):
                    pre_hb(tag, 2 * st + sub, GV[("pre_ffn", l)])
                sgi = 0
                for j in range(NJ):
                    s = j % NS
                    for sub in range(2):
                        bg = ps_next()
                        bu = ps_next()
                        for which, b in ((0, bg), (1, bu)):
                            for kc in range(NC8):
                                P.op("pe", lambda e, which=which, b=b, kc=kc, s=s, sub=sub: e.matmul(
                                    PS[b][:], WG[s][:, which, kc, :], HB[:, kc, sub * T:(sub + 1) * T],
                                    start=(kc == 0), stop=(kc == NC8 - 1)),
                                    r=[("wg", s), ("hb", sub, kc)], w=[("ps", b)])
                        q = sgi % 2
                        sgi += 1
                        P.op("act", lambda e, q=q, bg=bg: e.activation(out=SG[q][:], in_=PS[bg][:], func=AF.Silu),
                             r=[("ps", bg)], w=[("sg", q)])
                        P.op("dve", lambda e, q=q, bu=bu, j=j, sub=sub: e.tensor_tensor(
                            out=AA[:, j, sub * T:(sub + 1) * T], in0=SG[q][:], in1=PS[bu][:], op=ALU.mult),
                            r=[("sg", q), ("ps", bu)], w=[("aa", j, sub)])
                    if j + NS < NJ:
                        load_g(j + NS)
                    if j == NJ - 4:
                        load_d(0)
                        load_d(1)
                        load_d(2)
                if st == 0:
                    pre_hb(tag, 2, GV[("pre_ffn", l)])
                    pre_hb(tag, 3, GV[("pre_ffn", l)])
                elif hook is not None:
                    drain(hook(0))
                    drain(hook(1))
                for sub in range(2):
                    i = 2 * st + sub
                    for co in range(NC8):
                        k = sub * NC8 + co
                        s = k % 3
                        b = ps_next()
                        for kc in range(NJ):
                            P.op("pe", lambda e, b=b, kc=kc, s=s, sub=sub: e.matmul(
                                PS[b][:], WD[s][:, kc, :], AA[:, kc, sub * T:(sub + 1) * T], start=(kc == 0), stop=(kc == NJ - 1)),
                                r=[("wd", s), ("aa", kc, sub)], w=[("ps", b)])
                        if k + 3 < 2 * NC8:
                            load_d(k + 3)
                        evac_y(b, co)
                    postnorm_residual(i, GV[("post_ffn", l)])

    def ple(l, hook=None):
        tag = ("ple", l)
        with Scope(P, ["wga", "wpr", "ptt", "gt", "ysb", "sq8", "sqp8"]) as sc:
            YSB = sc.sb("YSB", [128, NC8, T], F32, "ysb")
            SQ8 = sc.sb("SQ8", [128, NC8, T], BF16, "sq8")
            SQP8 = sc.sb("SQP8", [128, NC8, T], BF16, "sqp8")
            GT = [sc.sb("GT%d" % s_, [128, T], F32, "gt") for s_ in range(2)]
            PTT = sc.sb("PTT", [128, 2, S], BF16, "ptt")
            WGA = sc.sb("WGA", [128, NC8, D], BF16, "wga")
            WPR = sc.sb("WPR", [128, 2, D], BF16, "wpr")
            P.dma("pool", lambda: (WGA[:, :, 0:512], wgate[l, :, :, 0:512]), "wga0", w=[("wga", 0)])
            P.dma("pool", lambda: (WPR[:], wproj[l, :, :, :]), "wpr", w=[("wpr",)])
            for ti in range(NT):
                P.dma("pool", lambda ti=ti: (PTT[:, :, ti * T:(ti + 1) * T], pT[l, :, :, ti * T:(ti + 1) * T]), "ptt%d" % ti, w=[("ptt", ti)])
            P.dma("pool", lambda: (WGA[:, :, 512:1024], wgate[l, :, :, 512:1024]), "wga1", w=[("wga", 1)])

            saved_cfg = {k: list(v) for k, v in statcfg.items()}
            statcfg[False] = [SQ8, "sq8", 8, 8]
            statcfg[True] = [SQP8, "sqp8", 8, 8]

            def main_g(i, ys, yk):
                q = i % 2
                cs = slice(i * T, (i + 1) * T)
                for co in range(NC8):
                    ba = ps_next()
                    bb = ps_next()
                    for kc in range(NC8):
                        P.op("pe", lambda e, kc=kc, co=co, ba=ba: e.matmul(
                            PS[ba][:], WGA[:, kc, co * 128:(co + 1) * 128], HB[:, kc, q * T:(q + 1) * T], start=(kc == 0), stop=(kc == NC8 - 1)),
                            r=[("wga", co // 4), ("hb", q, kc)], w=[("ps", ba)])
                    for kc in range(2):
                        P.op("pe", lambda e, kc=kc, co=co, bb=bb: e.matmul(
                            PS[bb][:], WPR[:, kc, co * 128:(co + 1) * 128], PTT[:, kc, cs], start=(kc == 0), stop=(kc == 1)),
                            r=[("wpr",), ("ptt", i)], w=[("ps", bb)])
                    g2 = co % 2
                    P.op("act", lambda e, g2=g2, ba=ba: e.activation(out=GT[g2][:], in_=PS[ba][:], func=AF.Sigmoid),
                         r=[("ps", ba)], w=[("gt", g2)])
                    P.op("dve", lambda e, g2=g2, bb=bb, co=co: e.tensor_tensor(out=ys[:, co, :], in0=GT[g2][:], in1=PS[bb][:], op=ALU.mult),
                         r=[("gt", g2), ("ps", bb)], w=[(yk, co)])
                    stat_add(co, lambda e, o, co=co: e.activation(out=o, in_=ys[:, co, :], func=AF.Square), [(yk, co)])
                    yield

            run_tiles(lambda i: pre_hb_g(tag, i, GV[("ple_g", l)]), main_g, GV[("ple_post", l)], hook, YSB)
            statcfg[False] = saved_cfg[False]
            statcfg[True] = saved_cfg[True]

    def kvproj(KT, VP, hook=None):
        with Scope(P, ["wkk", "wvv", "sqp8"]) as sc:
            sc.sb("SPC", [128, 15360], BF16)
            WKK = sc.sb("WKK", [128, NC8, 4, 128], BF16, "wkk")
            WVV = sc.sb("WVV", [128, NC8, 256], BF16, "wvv")
            SQP8 = sc.sb("SQP8k", [128, NC8, T], BF16, "sqp8")
            P.dma("pool", lambda: (WKK[:], wk[:, :, :, :]), "wkk", w=[("wkk",)])
            P.dma("pool", lambda: (WVV[:], wv[:, :, :]), "wvv", w=[("wvv",)])
            P.op("pool", lambda e: e.memset(VP[:], 0.0), w=[("vp", n) for n in range(16)])
            saved = list(statcfg[True])
            statcfg[True] = [SQP8, "sqp8", 8, 8]

            def main_g(i):
                q = i % 2
                cs = slice(i * T, (i + 1) * T)
                for kvh in range(4):
                    b = ps_next()
                    for kc in range(NC8):
                        P.op("pe", lambda e, kc=kc, kvh=kvh, b=b: e.matmul(
                            PS[b][:], WKK[:, kc, kvh, :], HB[:, kc, q * T:(q + 1) * T], start=(kc == 0), stop=(kc == NC8 - 1)),
                            r=[("wkk",), ("hb", q, kc)], w=[("ps", b)])
                    P.op("act", lambda e, kvh=kvh, b=b: e.activation(out=KT[:, kvh, cs], in_=PS[b][:], func=AF.Copy),
                         r=[("ps", b)], w=[("kt", i)])
                    yield
                for tb in range(4):
                    n = 4 * i + tb
                    b = ps_next()
                    for kc in range(NC8):
                        P.op("pe", lambda e, kc=kc, tb=tb, b=b: e.matmul(
                            PS[b][:, 0:256], HB[:, kc, q * T + tb * 128:q * T + (tb + 1) * 128], WVV[:, kc, :], start=(kc == 0), stop=(kc == NC8 - 1)),
                            r=[("wvv",), ("hb", q, kc)], w=[("ps", b)])
                    P.op("dve", lambda e, n=n, b=b: e.tensor_copy(
                        out=VP[:, n, 64:576].rearrange("p (k d) -> p k d", k=4)[:, :, 0:64], in_=PS[b][:, 0:256].rearrange("p (k d) -> p k d", k=4)),
                        r=[("ps", b)], w=[("vp", n)])
                    yield

            drain(pre_hb_g("kv", 0, GV["kv_g"]))
            for i in range(NT):
                if i + 1 < NT:
                    nxt = pre_hb_g("kv", i + 1, GV["kv_g"])
                else:
                    nxt = hook(0) if hook is not None else None
                bgs = [nxt] if nxt is not None else []
                for _ in main_g(i):
                    run_bg(bgs, 2)
                drain(*bgs)
            statcfg[True] = saved

    def mixer_attn(KT, VP, hook=None):
        slopes = [2.0 ** (-8.0 * (h + 1) / 16.0) for h in range(16)]
        NSL = 12
        with Scope(P, ["wqq", "woo", "qt", "at", "uu", "pt", "ll"]) as sc:
            QT = [sc.sb("QT0", [128, NC8, T], BF16, "qt")] * 2
            UU = [sc.sb("UU%d" % s_, [128, 256], F32, "uu") for s_ in range(4)]
            PT = [sc.sb("PT%d" % s_, [128, 256], BF16, "pt") for s_ in range(NSL)]
            LL = [sc.sb("LL%d" % s_, [128, 2, 128], F32, "ll") for s_ in range(2)]
            WQQ = sc.sb("WQQ", [128, NC8, D], BF16, "wqq")
            AT = sc.sb("AT", [128, NC8, T], BF16, "at")
            WOO = sc.sb("WOO", [128, NC8, D], BF16, "woo")
            P.dma("pool", lambda: (WQQ[:, 0:4, :], wq[:, 0:4, :]), "wqq", w=[("wqq",)])
            P.dma("pool", lambda: (WQQ[:, 4:8, :], wq[:, 4:8, :]), "wqq", w=[("wqq",)])
            P.dma("pool", lambda: (WOO[:, 0:4, :], wo[:, 0:4, :]), "woo", w=[("woo",)])
            P.dma("pool", lambda: (WOO[:, 4:8, :], wo[:, 4:8, :]), "woo", w=[("woo",)])
            gcnt = [0]

            def pre_a(i):
                pre_hb("attn", i, GV[("pre_mix", 1)])

            def pre_b_g(i, banks=None):
                q = 0
                hq = i % 2
                for co in range(NC8):
                    b = ps_next() if banks is None else banks[co % len(banks)]
                    for kc in range(NC8):
                        P.op("pe", lambda e, kc=kc, co=co, b=b: e.matmul(
                            PS[b][:], WQQ[:, kc, co * 128:(co + 1) * 128], HB[:, kc, hq * T:(hq + 1) * T], start=(kc == 0), stop=(kc == NC8 - 1)),
                            r=[("wqq",), ("hb", hq, kc)], w=[("ps", b)])
                    P.op("act", lambda e, co=co, b=b: e.activation(out=QT[q][:, co, :], in_=PS[b][:], func=AF.Copy),
                         r=[("ps", b)], w=[("qt", q, co)])
                    yield

            def main(i):
                qq = 0
                items = [(nq, k) for nq in range(4) for k in range(4)]
                base = gcnt[0]
                gcnt[0] += len(items)

                def geom(t):
                    nq, k = items[t]
                    n = 4 * i + nq
                    kbs = [n - 1, n] if n > 0 else [n]
                    g = base + t
                    return nq, k, n, kbs, 128 * len(kbs), g % 2, (g % 3) * 4

                def emit_S(t):
                    nq, k, n, kbs, width, par, s0 = geom(t)
                    for e4 in range(4):
                        j = 2 * k + e4 // 2
                        hh = e4 % 2
                        b = (0, 2)[par] + hh
                        off = (e4 // 2) * 256
                        pr = slice(hh * 64, (hh + 1) * 64)
                        for kbi, kb in enumerate(kbs):
                            P.op("pe", lambda e, b=b, pr=pr, k=k, kb=kb, kbi=kbi, j=j, nq=nq, off=off: e.matmul(
                                PS[b][:, off + kbi * 128:off + (kbi + 1) * 128], KT[pr, k, kb * 128:(kb + 1) * 128],
                                QT[qq][pr, j, nq * 128:(nq + 1) * 128], start=True, stop=True),
                                r=[("kt", kb // 4), ("qt", qq, j)], w=[("ps", b)])

                def emit_soft(t):
                    nq, k, n, kbs, width, par, s0 = geom(t)
                    mlo = 0 if len(kbs) == 2 else 128
                    for e4 in range(4):
                        h = 4 * k + e4
                        hh = e4 % 2
                        b = (0, 2)[par] + hh
                        off = (e4 // 2) * 256
                        sl = s0 + e4
                        ul = e4
                        P.op("dve", lambda e, b=b, sl=ul, h=h, mlo=mlo, width=width, off=off: e.scalar_tensor_tensor(
                            out=UU[sl][:, 0:width], in0=MT[:, mlo:mlo + width], scalar=slopes[h] / SCALE, in1=PS[b][:, off:off + width],
                            op0=ALU.mult, op1=ALU.add), r=[("mt",), ("ps", b)], w=[("uu", ul)])
                        P.op("act", lambda e, sl=sl, ul=ul, h=h, width=width: e.activation(
                            out=PT[sl][:, 0:width], in_=UU[ul][:, 0:width], func=AF.Exp, bias=NSINK[:, h:h + 1], scale=SCALE),
                            r=[("uu", ul), ("nsink",)], w=[("pt", sl)])

                def emit_PV(t):
                    nq, k, n, kbs, width, par, s0 = geom(t)
                    bo = 4 + par
                    nmm = 2 * len(kbs)
                    for pp in range(2):
                        for which in range(2):
                            m = 0
                            c0 = pp * 256 + which * 128
                            for hh in range(2):
                                sl = s0 + pp * 2 + hh
                                for kbi, kb in enumerate(kbs):
                                    if which == 0:
                                        lf = (lambda kb=kb, k=k, hh=hh: VP[:, kb, 64 + 128 * k:192 + 128 * k] if hh == 0 else VP[:, kb, 128 * k:128 * k + 128])
                                        rk = [("vp", kb)]
                                    else:
                                        lf = (lambda hh=hh: OP[:, 64:192] if hh == 0 else OP[:, 0:128])
                                        rk = [("op",)]
                                    P.op("pe", lambda e, lf=lf, bo=bo, c0=c0, sl=sl, kbi=kbi, m=m: e.matmul(
                                        PS[bo][:, c0:c0 + 128], lf(), PT[sl][:, kbi * 128:(kbi + 1) * 128],
                                        start=(m == 0), stop=(m == nmm - 1)),
                                        r=rk + [("pt", sl)], w=[("ps", bo)])
                                    m += 1

                def emit_norm_act(t):
                    nq, k, n, kbs, width, par, s0 = geom(t)
                    bo = 4 + par
                    ov = lambda: PS[bo][:, :].rearrange("p (a b c) -> p a b c", a=2, b=2)
                    P.op("act", lambda e: e.activation(out=LL[par][:], in_=ov()[:, :, 1, :], func=AF.Ln, bias=EPST[:, 1:2], scale=1.0),
                         r=[("ps", bo), ("eps",)], w=[("ll", par)])
                    P.op("act", lambda e: e.activation(out=LL[par][:], in_=LL[par][:], func=AF.Exp, scale=-1.0),
                         r=[("ll", par)], w=[("ll", par)])

                def emit_norm_dve(t):
                    nq, k, n, kbs, width, par, s0 = geom(t)
                    bo = 4 + par
                    ov = lambda: PS[bo][:, :].rearrange("p (a b c) -> p a b c", a=2, b=2)
                    P.op("dve", lambda e: e.tensor_tensor(
                        out=AT[:, 2 * k:2 * k + 2, nq * 128:(nq + 1) * 128], in0=ov()[:, :, 0, :], in1=LL[par][:], op=ALU.mult),
                        r=[("ps", bo), ("ll", par)], w=[("at", 2 * k), ("at", 2 * k + 1)])

                ni = len(items)
                for st_ in range(ni + 3):
                    if 0 <= st_ - 3 < ni:
                        emit_norm_act(st_ - 3)
                    if st_ < ni:
                        emit_S(st_)
                    if 0 <= st_ - 1 < ni:
                        emit_soft(st_ - 1)
                    if 0 <= st_ - 2 < ni:
                        emit_PV(st_ - 2)
                    if 0 <= st_ - 3 < ni:
                        emit_norm_dve(st_ - 3)
                    yield st_

            def oproj(i):
                for co in range(NC8):
                    b = ps_next()
                    for kc in range(NC8):
                        P.op("pe", lambda e, kc=kc, co=co, b=b: e.matmul(
                            PS[b][:], WOO[:, kc, co * 128:(co + 1) * 128], AT[:, kc, :], start=(kc == 0), stop=(kc == NC8 - 1)),
                            r=[("woo",), ("at", kc)], w=[("ps", b)])
                    evac_y(b, co)

            pre_a(0)
            drain(pre_b_g(0))
            pa1 = pre_hb_g("attn", 1, GV[("pre_mix", 1)])
            pend = None
            for i in range(NT):
                qg = pre_b_g(i + 1, banks=[0, 1, 2, 3]) if i + 1 < NT else None
                for st_ in main(i):
                    if i == 0 and pa1 is not None and st_ < 14:
                        run_bg([pa1], 1)
                    if pend is not None:
                        run_bg([pend], 1)
                    if qg is not None and st_ >= 16:
                        run_bg([qg], 3)
                drain(pend, qg)
                if i + 2 < NT:
                    pre_a(i + 2)
                elif hook is not None:
                    drain(hook(i + 2 - NT))
                oproj(i)
                pend = postnorm_begin(i, GV[("post_mix", 1)])
            drain(pend)

    full = stop_after is None
    mixer_pool(hook=lambda k: pre_hb_g(("ffn", 0), k, GV[("pre_ffn", 0)]))
    if stop_after == "pool":
        return store_out()
    ffn(0, hook=lambda k: pre_hb_g(("ple", 0), k, GV[("ple_g", 0)]))
    if stop_after == "ffn0":
        return store_out()
    ple(0, hook=(lambda k: pre_hb_g("kv", k, GV["kv_g"])) if full else None)
    if stop_after == "ple0":
        return store_out()
    with Scope(P, ["kt", "vp"]) as kvs:
        KT = kvs.sb("KT", [128, 4, S], BF16, "kt")
        VP = kvs.sb("VP", [128, 16, 576], BF16, "vp")
        kvproj(KT, VP, hook=lambda k: pre_hb_g("attn", k, GV[("pre_mix", 1)]))
        mixer_attn(KT, VP, hook=(lambda k: pre_hb_g(("ffn", 1), k, GV[("pre_ffn", 1)])) if full else None)
    if stop_after == "attn":
        return store_out()
    ffn(1, hook=lambda k: pre_hb_g(("ple", 1), k, GV[("ple_g", 1)]))
    ple(1)
    return store_out()


_CACHE = {}


def get_program(stop_after=None):
    if stop_after not in _CACHE:
        _, Pd = build(True, None, stop_after)
        nc, Pr = build(False, Pd.need, stop_after)
        _CACHE[stop_after] = nc
    return _CACHE[stop_after]


def prep_shared(inp):
    f = lambda a: np.ascontiguousarray(a, dtype=np.float32)
    sh = {}
    vecs = []
    for l in range(2):
        for n in ["pre_mix_g", "post_mix_g", "pre_ffn_g", "post_ffn_g", "ple_g", "ple_post_g"]:
            vecs.append(inp[n][l])
    vecs.append(inp["pool_scale"][0])
    vecs.append(inp["kv_g"])
    g = np.stack([np.asarray(v, np.float32) for v in vecs], 0)
    sh["gains"] = f(g.reshape(NV, NC8, 128).transpose(2, 0, 1))
    sh["sinkb"] = f(np.broadcast_to(np.asarray(inp["sinks"], np.float32).reshape(1, 16), (128, 16)))
    ki = np.arange(128)[:, None]
    qr = np.arange(256)[None, :]
    rel = qr - ki
    m = np.where((rel >= 0) & (rel < 128), -rel.astype(np.float32), -1.0e6)
    sh["maskT"] = f(np.concatenate([m[:, 128:256], m[:, 0:128]], axis=1))
    ic = np.zeros((128, 4, 2, 16), np.float32)
    for gi, w in enumerate((2, 4, 8, 16)):
        ic[:, gi, :, :] = 1.0 / np.minimum(np.arange(1, 17), w).astype(np.float32)
    sh["invc"] = ic
    pw = np.asarray(inp["pool_w"], np.float32)[0]
    sh["wpool"] = f(pw.reshape(4, 2, 128, 256).transpose(2, 0, 1, 3))
    wgu = np.asarray(inp["w_gu"], np.float32)
    sh["wgu"] = f(wgu.reshape(2, NC8, 128, 2, NJ, 128).transpose(0, 4, 2, 3, 1, 5))
    wd = np.asarray(inp["w_down"], np.float32)
    sh["wdn"] = f(wd.reshape(2, NJ, 128, NC8, 128).transpose(0, 3, 2, 1, 4))
    wg = np.asarray(inp["w_ple_gate"], np.float32)
    sh["wgate"] = f(wg.reshape(2, NC8, 128, D).transpose(0, 2, 1, 3))
    wp = np.asarray(inp["w_ple_proj"], np.float32)
    sh["wproj"] = f(wp.reshape(2, 2, 128, D).transpose(0, 2, 1, 3))
    sh["wq"] = f(np.asarray(inp["w_q"], np.float32)[0].reshape(NC8, 128, D).transpose(1, 0, 2))
    sh["wo"] = f(np.asarray(inp["w_o"], np.float32)[0].reshape(NC8, 128, D).transpose(1, 0, 2))
    wkv = np.asarray(inp["w_kv"], np.float32)
    k4 = wkv[:, :256].reshape(NC8, 128, 4, 64).transpose(1, 0, 2, 3)
    sh["wk"] = f(np.concatenate([k4, k4], axis=3))
    sh["wv"] = f(wkv[:, 256:].reshape(NC8, 128, 256).transpose(1, 0, 2))
    return sh


def kernel(**inp):
    stop_after = inp.pop("_stop_after", None)
    nc = get_program(stop_after)
    sh = prep_shared(inp)
    x = np.asarray(inp["x"], np.float32)
    p = np.asarray(inp["p"], np.float32)
    in_maps = []
    for b in range(8):
        m = dict(sh)
        m["xT"] = np.ascontiguousarray(x[b].T.reshape(NC8, 128, S).transpose(1, 0, 2))
        m["pT"] = np.ascontiguousarray(p[:, b].transpose(0, 2, 1).reshape(2, 2, 128, S).transpose(0, 2, 1, 3))
        in_maps.append(m)
    res = run_bass_kernel_spmd(nc, in_maps, core_ids=list(range(8)))
    out = np.empty((8, S, D), np.float32)
    for b in range(8):
        o = res.results[b]["outT"]
        out[b] = o.transpose(1, 0, 2).reshape(D, S).T
    return out
```

```python
import numpy as np
from contextlib import ExitStack
import concourse.bass as bass
import concourse.mybir as mybir
from concourse.bass_utils import run_bass_kernel_spmd

F32 = mybir.dt.float32
BF16 = mybir.dt.bfloat16
AF = mybir.ActivationFunctionType
ALU = mybir.AluOpType

S = 2048
D = 1024
NC8 = 8
T = 512
NT = S // T
DFF = 2816
NJ = DFF // 128
EPS = 1e-6
SCALE = 0.125
ENGS = ("pe", "act", "dve", "pool", "sp")

GV = {}
for _l in range(2):
    for _k, _n in enumerate(["pre_mix", "post_mix", "pre_ffn", "post_ffn", "ple_g", "ple_post"]):
        GV[(_n, _l)] = _l * 6 + _k
GV["pool_scale"] = 12
GV["kv_g"] = 13
NV = 14


class Prog:
    def __init__(self, nc, dry, sig):
        self.nc = nc
        self.dry = dry
        self.sig = sig
        self.need = {e: set() for e in ENGS}
        self.n = {e: 0 for e in ENGS}
        self.cnt = {e: 0 for e in ENGS}
        self.val = {e: {} for e in ENGS}
        self.waited = {e: {} for e in ENGS}
        self.last_w = {}
        self.readers = {}
        self.retired = []
        self.bufrange = {}
        self.seen = set()
        self.sems = {}
        self.semtot = {}
        self.es = ExitStack()
        if not dry:
            self.h = {"pe": nc.tensor, "act": nc.scalar, "dve": nc.vector, "pool": nc.gpsimd, "sp": nc.sync}
            self.esem = {e: self.es.enter_context(nc.semaphore("s_" + e)) for e in ENGS}
        self.nwaits = 0

    def sem(self, name):
        if name not in self.sems:
            self.sems[name] = None if self.dry else self.es.enter_context(self.nc.semaphore("d_" + name))
            self.semtot[name] = 0
        return self.sems[name]

    def _deps(self, r, w):
        deps = {}
        rdeps = {}

        def add(dst, d):
            k = (d[0], d[1])
            o = dst.get(k)
            if o is None or o[2] < d[2]:
                dst[k] = d
        for k in list(r) + list(w):
            if k not in self.seen:
                self.seen.add(k)
                mine = self.bufrange.get(k[0])
                for rng, ds in self.retired:
                    hit = True
                    if rng is not None and mine is not None:
                        hit = any(a0 < b1 and b0 < a1 for (a0, a1) in rng for (b0, b1) in mine)
                    if hit:
                        for d in ds:
                            if self.dry:
                                rdeps[d] = d
                            else:
                                add(rdeps, d)
            d = self.last_w.get(k)
            if d is not None:
                add(deps, d)
        for k in w:
            for d in self.readers.get(k, ()):
                add(deps, d)
        return list(deps.values()) + list(rdeps.values())

    def _update(self, me, r, w):
        for k in w:
            self.last_w[k] = me
            self.readers[k] = []
        for k in r:
            if k in w:
                continue
            self.readers.setdefault(k, []).append(me)

    def _emit_waits(self, eng, deps):
        want = {}
        for d in deps:
            if d[0] == "c":
                _, pe_, idx = d
                if pe_ == eng and eng == "pe":
                    continue
                if self.dry:
                    self.need[pe_].add(idx)
                    continue
                v = self.val[pe_][idx]
                key = ("c", pe_)
                want[key] = max(want.get(key, 0), v)
            else:
                _, sname, v = d
                key = ("d", sname)
                want[key] = max(want.get(key, 0), v)
        if self.dry:
            return
        for key, v in want.items():
            if self.waited[eng].get(key, 0) >= v:
                continue
            self.waited[eng][key] = v
            s = self.esem[key[1]] if key[0] == "c" else self.sems[key[1]]
            self.h[eng].wait_ge(s, v)
            self.nwaits += 1

    def op(self, eng, fn, r=(), w=()):
        deps = self._deps(r, w)
        idx = self.n[eng]
        self.n[eng] += 1
        me = ("c", eng, idx)
        self._emit_waits(eng, deps)
        if not self.dry:
            ins = fn(self.h[eng])
            if idx in self.sig[eng]:
                self.cnt[eng] += 1
                ins.then_inc(self.esem[eng], 1)
                self.val[eng][idx] = self.cnt[eng]
        self._update(me, r, w)
        return me

    def dma(self, q, fn, semname, r=(), w=()):
        deps = self._deps(r, w)
        self.n[q] += 1
        self._emit_waits(q, deps)
        s = self.sem(semname)
        self.semtot[semname] += 16
        me = ("d", semname, self.semtot[semname])
        if not self.dry:
            o, i = fn()
            self.h[q].dma_start(out=o, in_=i).then_inc(s, 16)
        self._update(me, r, w)
        return me

    def retire(self, prefixes):
        per = {}
        for k in list(self.seen):
            if k[0] in prefixes:
                d = self.last_w.pop(k, None)
                ds = self.readers.pop(k, [])
                if d is not None:
                    ds = ds + [d]
                best = per.setdefault(k[0], {})
                for d in ds:
                    key = (d[0], d[1])
                    old = best.get(key)
                    if old is None or old[2] < d[2]:
                        best[key] = d
                self.seen.discard(k)
        for pfx, best in per.items():
            self.retired.append((self.bufrange.pop(pfx, None), list(best.values())))
        for pfx in prefixes:
            self.bufrange.pop(pfx, None)

    def finish(self):
        deps = [("d", n, v) for n, v in self.semtot.items() if v > 0]
        for e in ENGS:
            if e != "sp" and self.n[e] > 0:
                pass
        self._emit_waits("sp", deps)


class Scope:
    uid = 0

    def __init__(self, P, prefixes):
        self.P = P
        self.prefixes = prefixes
        self.es = ExitStack()

    def sb(self, name, shape, dt, kp=None):
        if self.P.dry:
            return None
        Scope.uid += 1
        t = self.es.enter_context(self.P.nc.sbuf_tensor("%s_%d" % (name, Scope.uid), shape, dt))
        if kp is not None:
            ml = self.P.nc.lookup_mloc(t)
            self.P.bufrange.setdefault(kp, []).append((int(ml.addr), int(ml.addr) + int(ml.dims[1])))
        return t

    def __enter__(self):
        return self

    def __exit__(self, *a):
        self.P.retire(self.prefixes)
        self.es.close()
        return False


def build(dry, sig, stop_after=None):
    nc = None if dry else bass.Bass("TRN2", target_bir_lowering=False)
    P = Prog(nc, dry, sig)

    def dram(name, shape, dt=F32, kind="ExternalInput"):
        if dry:
            return None
        return nc.dram_tensor(name, shape, dt, kind=kind).ap()

    xT = dram("xT", [128, NC8, S])
    pT = dram("pT", [2, 128, 2, S])
    gains = dram("gains", [128, NV, NC8])
    sinkb = dram("sinkb", [128, 16])
    maskT = dram("maskT", [128, 256])
    invc = dram("invc", [128, 4, 2, 16])
    wpool = dram("wpool", [128, 4, 2, 256])
    wgu = dram("wgu", [2, NJ, 128, 2, NC8, 128])
    wdn = dram("wdn", [2, NC8, 128, NJ, 128])
    wgate = dram("wgate", [2, 128, NC8, D])
    wproj = dram("wproj", [2, 128, 2, D])
    wq = dram("wq", [128, NC8, D])
    wo = dram("wo", [128, NC8, D])
    wk = dram("wk", [128, NC8, 4, 128])
    wv = dram("wv", [128, NC8, 256])
    outT = dram("outT", [128, NC8, S], kind="ExternalOutput")

    top = Scope(P, [])
    X = top.sb("X", [128, NC8, S], F32)
    G = top.sb("G", [128, NV, NC8], F32)
    ONES = top.sb("ONES", [128, 128], BF16)
    OP = top.sb("OP", [128, 192], BF16)
    EPST = top.sb("EPST", [128, 2], F32)
    NSINK = top.sb("NSINK", [128, 16], F32)
    MT = top.sb("MT", [128, 256], F32)
    INVC = top.sb("INVC", [128, 4, 2, 16], F32)
    HB = top.sb("HB", [128, NC8, 2 * T], BF16)
    NSQ = 3
    SQ = top.sb("SQ", [128, NSQ, T], BF16)
    RS = top.sb("RS", [128, T], F32)
    SQP = top.sb("SQP", [128, NSQ, T], BF16)
    RSP = top.sb("RSP", [128, T], F32)
    YS = top.sb("YS", [128, NC8, T], F32)
    TMP2 = [top.sb("TMP%d" % q_, [128, T], F32) for q_ in range(2)]
    PS = [None] * 8
    if not dry:
        for b in range(8):
            PS[b] = top.es.enter_context(nc.psum_tensor("ps%d" % b, [128, 512], F32))

    psrot = [0]

    def ps_next():
        b = psrot[0]
        psrot[0] = (b + 1) % 6
        return b

    strot = [0]

    def ps_stat():
        b = 6 + strot[0]
        strot[0] ^= 1
        return b

    P.dma("sp", lambda: (G[:], gains[:, :, :]), "cstG", w=[("G",)])
    P.dma("sp", lambda: (NSINK[:], sinkb[:, :]), "cstS", w=[("nsink",)])
    P.dma("sp", lambda: (MT[:], maskT[:, :]), "cstM", w=[("mt",)])
    P.dma("sp", lambda: (INVC[:], invc[:, :, :, :]), "cstI", w=[("invc",)])
    for i in range(NT):
        P.dma("sp", lambda i=i: (X[:, :, i * T:(i + 1) * T], xT[:, :, i * T:(i + 1) * T]), "x%d" % i,
              r=([("x", 0)] if i >= 1 else []), w=[("x", i)])
    P.op("dve", lambda e: e.memset(ONES[:], 1.0), w=[("ones",)])
    P.op("dve", lambda e: e.memset(OP[:], 0.0), w=[("op",)])
    P.op("dve", lambda e: e.memset(OP[:, 64:128], 1.0), w=[("op",)])
    P.op("dve", lambda e: e.memset(EPST[:, 0:1], EPS), w=[("eps",)])
    P.op("dve", lambda e: e.memset(EPST[:, 1:2], 1.0), w=[("eps",)])
    P.op("pool", lambda e: e.tensor_scalar(out=NSINK[:], in0=NSINK[:], scalar1=-1.0, scalar2=None, op0=ALU.mult),
         r=[("nsink",)], w=[("nsink",)])

    sqrot = {False: [0], True: [0]}
    statbank = {False: [None], True: [None]}

    statq = {False: [], True: []}
    statcfg = {False: [SQ, "sq", NSQ, 2], True: [SQP, "sqp", NSQ, 2]}

    def stat_add(c, sq_fn, rkeys, pre=False):
        sq, sk, nsq, lag = statcfg[pre]
        if c == 0:
            statbank[pre][0] = ps_stat()
            sqrot[pre][0] = 0
        b = statbank[pre][0]
        sl = sqrot[pre][0]
        sqrot[pre][0] = (sl + 1) % nsq
        P.op("act", lambda e: sq_fn(e, sq[:, sl, :]), r=rkeys, w=[(sk, sl)])
        statq[pre].append(lambda: P.op(
            "pe", lambda e: e.matmul(PS[b][:], ONES[:], sq[:, sl, :], start=(c == 0), stop=(c == NC8 - 1)),
            r=[("ones",), (sk, sl)], w=[("ps", b)]))
        while len(statq[pre]) > lag:
            statq[pre].pop(0)()

    def stats_to_rs(pre=False):
        rs, rk = (RSP, "rsp") if pre else (RS, "rs")
        while statq[pre]:
            statq[pre].pop(0)()
        b = statbank[pre][0]
        P.op("act", lambda e, b=b: e.activation(out=rs[:], in_=PS[b][:], func=AF.Ln, bias=EPST[:, 0:1], scale=1.0 / D),
             r=[("ps", b), ("eps",)], w=[(rk,)])
        P.op("act", lambda e: e.activation(out=rs[:], in_=rs[:], func=AF.Exp, scale=-0.5), r=[(rk,)], w=[(rk,)])

    def drain(*gens):
        gens = [g for g in gens if g is not None]
        alive = True
        while alive:
            alive = False
            for g in gens:
                try:
                    next(g)
                    alive = True
                except StopIteration:
                    pass

    def run_bg(gens, k=1):
        for g in gens:
            for _ in range(k):
                try:
                    next(g)
                except StopIteration:
                    break

    def prenorm_g(i, v, out_fn, out_keys):
        cs = slice(i * T, (i + 1) * T)
        for c in range(NC8):
            stat_add(c, lambda e, o, c=c: e.activation(out=o, in_=X[:, c, cs], func=AF.Square), [("x", i)], pre=True)
            if c % 2 == 1:
                yield
        stats_to_rs(pre=True)
        yield
        for c in range(NC8):
            P.op("dve", lambda e, c=c: e.scalar_tensor_tensor(out=out_fn(c), in0=X[:, c, cs], scalar=G[:, v, c:c + 1],
                                                               in1=RSP[:], op0=ALU.mult, op1=ALU.mult),
                 r=[("x", i), ("G",), ("rsp",)], w=[out_keys(c)])
            if c % 2 == 1:
                yield

    def prenorm(i, v, out_fn, out_keys):
        drain(prenorm_g(i, v, out_fn, out_keys))

    hoisted = set()

    def pre_hb_g(tag, i, v):
        if (tag, i) in hoisted:
            return None
        hoisted.add((tag, i))
        q = i % 2
        return prenorm_g(i, v, lambda c: HB[:, c, q * T:(q + 1) * T], lambda c: ("hb", q, c))

    def pre_hb(tag, i, v):
        drain(pre_hb_g(tag, i, v))

    def pipeline(n, pre, main, post, hook=None):
        for i in range(min(2, n)):
            pre(i)
        for i in range(n):
            main(i)
            if i + 2 < n:
                pre(i + 2)
            elif hook is not None:
                drain(hook(i + 2 - n))
            post(i)

    def postnorm_begin(i, v, ys=None, yk="ys", fine=False):
        ys = YS if ys is None else ys
        cs = slice(i * T, (i + 1) * T)
        while statq[False]:
            statq[False].pop(0)()
        b = statbank[False][0]

        def gen():
            P.op("act", lambda e: e.activation(out=RS[:], in_=PS[b][:], func=AF.Ln, bias=EPST[:, 0:1], scale=1.0 / D),
                 r=[("ps", b), ("eps",)], w=[("rs",)])
            P.op("act", lambda e: e.activation(out=RS[:], in_=RS[:], func=AF.Exp, scale=-0.5), r=[("rs",)], w=[("rs",)])
            yield
            for c in range(NC8):
                q = c % 2
                P.op("dve", lambda e, c=c, q=q: e.scalar_tensor_tensor(out=TMP2[q][:], in0=ys[:, c, :], scalar=G[:, v, c:c + 1],
                                                                        in1=RS[:], op0=ALU.mult, op1=ALU.mult),
                     r=[(yk, c), ("G",), ("rs",)], w=[("tmp", q)])
                if fine:
                    yield
                P.op("dve", lambda e, c=c, q=q: e.tensor_tensor(out=X[:, c, cs], in0=X[:, c, cs], in1=TMP2[q][:], op=ALU.add),
                     r=[("tmp", q)], w=[("x", i)])
                yield
        return gen()

    def postnorm_residual(i, v):
        drain(postnorm_begin(i, v))

    def evac_y(b, c, scale_ap=None, ys=None, yk="ys"):
        ys = YS if ys is None else ys
        if scale_ap is None:
            P.op("act", lambda e: e.activation(out=ys[:, c, :], in_=PS[b][:], func=AF.Copy), r=[("ps", b)], w=[(yk, c)])
            stat_add(c, lambda e, o: e.activation(out=o, in_=PS[b][:], func=AF.Square), [("ps", b)])
        else:
            P.op("act", lambda e: e.activation(out=ys[:, c, :], in_=PS[b][:], func=AF.Copy, scale=scale_ap()),
                 r=[("ps", b), ("G",)], w=[(yk, c)])
            stat_add(c, lambda e, o: e.activation(out=o, in_=PS[b][:], func=AF.Square, scale=scale_ap()), [("ps", b), ("G",)])

    def run_tiles(pre_g, main_g, post_v, hook, YSB):
        drain(pre_g(0))
        pend = None
        for i in range(NT):
            ys, yk = (YS, "ys") if i % 2 == 1 else (YSB, "ysb")
            bgs = [pend] if pend is not None else []
            if i + 1 < NT:
                nxt = pre_g(i + 1)
            else:
                nxt = hook(0) if hook is not None else None
            if nxt is not None:
                bgs.append(nxt)
            for _ in main_g(i, ys, yk):
                run_bg(bgs, 2)
            drain(*bgs)
            pend = postnorm_begin(i, post_v, ys, yk)
        drain(pend, hook(1) if hook is not None else None)

    def store_out():
        for i in range(NT):
            P.dma("sp", lambda i=i: (outT[:, :, i * T:(i + 1) * T], X[:, :, i * T:(i + 1) * T]), "out%d" % i, r=[("x", i)])
        P.finish()
        top.es.close()
        P.es.close()
        return nc, P

    def mixer_pool(hook=None):
        wins = (2, 4, 8, 16)
        with Scope(P, ["hf", "wp", "wpa", "wpb", "a32", "s2", "s4", "s8", "s16", "e16", "t16", "ysb", "sq8", "sqp8", "s2b", "s4b"]) as sc:
            W = 16 + T
            HF = [sc.sb("HFb%d" % q, [128, NC8, W], BF16, "hf") for q in range(2)]
            WP = sc.sb("WP", [128, 4, 2, 256], BF16, "wp")
            WPA = sc.sb("WPA", [128, 4, 2, 256], BF16, "wpa")
            WPB = sc.sb("WPB", [128, 4, 2, 256], BF16, "wpb")
            S2B = sc.sb("S2B", [128, 4, W], BF16, "s2b")
            S4B = [sc.sb("S4B%d" % q, [128, 4, W], BF16, "s4b") for q in range(2)]
            A32 = sc.sb("A32", [128, 8, 32], F32, "a32")
            S2 = sc.sb("S2", [128, 8, 32], F32, "s2")
            S4 = sc.sb("S4", [128, 6, 32], F32, "s4")
            S8 = sc.sb("S8", [128, 4, 32], F32, "s8")
            S16 = sc.sb("S16", [128, 2, 32], F32, "s16")
            E16 = sc.sb("E16", [128, 8, 16], BF16, "e16")
            T16 = sc.sb("T16", [128, 2, 16], F32, "t16")
            YSB = sc.sb("YSBp", [128, NC8, T], F32, "ysb")
            SQ8 = sc.sb("SQ8p", [128, NC8, T], BF16, "sq8")
            SQP8 = sc.sb("SQP8p", [128, NC8, T], BF16, "sqp8")
            P.dma("pool", lambda: (WP[:], wpool[:, :, :, :]), "wp", w=[("wp",)])
            for g in range(4):
                w_ = wins[g]
                P.op("dve", lambda e, g=g, w_=w_: e.tensor_scalar(out=WPA[:, g], in0=WP[:, g], scalar1=1.0 / w_, scalar2=None, op0=ALU.mult),
                     r=[("wp",)], w=[("wpa", g)])
                P.op("dve", lambda e, g=g, w_=w_: e.tensor_scalar(out=WPB[:, g], in0=WP[:, g], scalar1=(1.0 / w_ - 1.0) if g < 2 else -1.0, scalar2=None, op0=ALU.mult),
                     r=[("wp",)], w=[("wpb", g)])
            pv = GV["pool_scale"]

            def pre_g(i):
                q = i % 2
                hf = HF[q]
                if i == 0:
                    P.op("dve", lambda e: e.memset(hf[:, :, 0:16], 0.0), w=[("hf", q)])
                else:
                    P.op("dve", lambda e: e.tensor_copy(out=hf[:, :, 0:16], in_=HF[1 - q][:, :, T:T + 16]),
                         r=[("hf", 1 - q)], w=[("hf", q)])
                yield from prenorm_g(i, GV[("pre_mix", 0)], lambda c: hf[:, c, 16:W], lambda c: ("hf", q))
                P.op("dve", lambda e: e.tensor_tensor(out=S2B[:, :, 1:W], in0=hf[:, 4:8, 1:W], in1=hf[:, 4:8, 0:W - 1], op=ALU.add),
                     r=[("hf", q)], w=[("s2b",)])
                P.op("dve", lambda e: e.tensor_tensor(out=S4B[q][:, :, 3:W], in0=S2B[:, :, 3:W], in1=S2B[:, :, 1:W - 2], op=ALU.add),
                     r=[("s2b",)], w=[("s4b", q)])
                yield
                if i == 0:
                    P.op("dve", lambda e: e.tensor_copy(out=A32[:], in_=hf[:, :, 0:32]), r=[("hf", q)], w=[("a32",)])
                    P.op("dve", lambda e: e.tensor_tensor(out=S2[:, :, 1:32], in0=A32[:, :, 1:32], in1=A32[:, :, 0:31], op=ALU.add),
                         r=[("a32",)], w=[("s2",)])
                    P.op("dve", lambda e: e.tensor_tensor(out=S4[:, :, 3:32], in0=S2[:, 2:8, 3:32], in1=S2[:, 2:8, 1:30], op=ALU.add),
                         r=[("s2",)], w=[("s4",)])
                    P.op("dve", lambda e: e.tensor_tensor(out=S8[:, :, 7:32], in0=S4[:, 2:6, 7:32], in1=S4[:, 2:6, 3:28], op=ALU.add),
                         r=[("s4",)], w=[("s8",)])
                    P.op("dve", lambda e: e.tensor_tensor(out=S16[:, :, 15:32], in0=S8[:, 2:4, 15:32], in1=S8[:, 2:4, 7:24], op=ALU.add),
                         r=[("s8",)], w=[("s16",)])
                    srcs = [(lambda: S2[:, 0:2, 16:32], ("s2",)), (lambda: S4[:, 0:2, 16:32], ("s4",)),
                            (lambda: S8[:, 0:2, 16:32], ("s8",)), (lambda: S16[:, 0:2, 16:32], ("s16",))]
                    for g in range(4):
                        sf, sk = srcs[g]
                        P.op("dve", lambda e, g=g, sf=sf: e.tensor_tensor(out=T16[:], in0=sf(), in1=INVC[:, g, :, :], op=ALU.mult),
                             r=[sk, ("invc",)], w=[("t16",)])
                        P.op("dve", lambda e, g=g: e.tensor_tensor(out=E16[:, 2 * g:2 * g + 2, :], in0=T16[:],
                                                                  in1=A32[:, 2 * g:2 * g + 2, 16:32], op=ALU.subtract),
                             r=[("t16",), ("a32",)], w=[("e16", g)])

            def main_g(i, ys, yk):
                q = i % 2
                hf = HF[q]
                for g in range(4):
                    w_ = wins[g]
                    for cc in range(2):
                        b = ps_next()
                        c = 2 * g + cc
                        terms = []
                        for kc in range(2):
                            if g < 2:
                                for k in range(w_):
                                    terms.append(((WPB if k == 0 else WPA), kc,
                                                  (lambda kc=kc, k=k: hf[:, 2 * g + kc, 16 - k:16 - k + T]),
                                                  [("wpb" if k == 0 else "wpa", g), ("hf", q)]))
                            else:
                                terms.append((WPB, kc, (lambda kc=kc: hf[:, 2 * g + kc, 16:16 + T]), [("wpb", g), ("hf", q)]))
                                for m4 in range(w_ // 4):
                                    terms.append((WPA, kc,
                                                  (lambda kc=kc, m4=m4: S4B[q][:, 2 * g + kc - 4, 16 - 4 * m4:16 - 4 * m4 + T]),
                                                  [("wpa", g), ("s4b", q)]))
                        nmm = len(terms)
                        m = 0
                        for wt, kc, rf, rk in terms:
                            P.op("pe", lambda e, g=g, cc=cc, kc=kc, b=b, wt=wt, rf=rf, m=m, nmm=nmm: e.matmul(
                                PS[b][:], wt[:, g, kc, cc * 128:(cc + 1) * 128], rf(), start=(m == 0), stop=(m == nmm - 1)),
                                r=rk, w=[("ps", b)])
                            m += 1
                        if i == 0:
                            for kc in range(2):
                                P.op("pe", lambda e, g=g, cc=cc, kc=kc, b=b, m=m, nmm=nmm: e.matmul(
                                    PS[b][:, 0:16], WP[:, g, kc, cc * 128:(cc + 1) * 128], E16[:, 2 * g + kc, :],
                                    start=(kc == 0), stop=(kc == 1)),
                                    r=[("wp",), ("e16", g)], w=[("ps", b)])
                                m += 1
                        evac_y(b, c, scale_ap=lambda c=c: G[:, pv, c:c + 1], ys=ys, yk=yk)
                        yield

            saved_cfg = {k: list(v) for k, v in statcfg.items()}
            statcfg[False] = [SQ8, "sq8", 8, 8]
            statcfg[True] = [SQP8, "sqp8", 8, 8]
            run_tiles(pre_g, main_g, GV[("post_mix", 0)], hook, YSB)
            statcfg[False] = saved_cfg[False]
            statcfg[True] = saved_cfg[True]

    def ffn(l, hook=None):
        NS = 4
        tag = ("ffn", l)
        with Scope(P, ["aa", "wg", "wd", "sg"]) as sc:
            AA = sc.sb("AA", [128, NJ, 2 * T], BF16, "aa")
            WG = [sc.sb("WG%d" % s, [128, 2, NC8, 128], BF16, "wg") for s in range(NS)]
            WD = [sc.sb("WD%d" % s, [128, NJ, 128], BF16, "wd") for s in range(3)]
            SG = [sc.sb("SG%d" % s, [128, T], F32, "sg") for s in range(2)]
            for st in range(2):
                def load_g(j):
                    s = j % NS
                    P.dma("pool", lambda: (WG[s][:], wgu[l, j, :, :, :, :]), "wg%d" % s, w=[("wg", s)])

                def load_d(k):
                    s = k % 3
                    P.dma("pool", lambda: (WD[s][:], wdn[l, k % NC8, :, :, :]), "wd%d" % s, w=[("wd", s)])

                for j in range(NS):
                    load_g(j)
                for sub in range(2):
                    pre_hb(tag, 2 * st + sub, GV[("pre_ffn", l)])
                sgi = 0
                for j in range(NJ):
                    s = j % NS
                    for sub in range(2):
                        bg = ps_next()
                        bu = ps_next()
                        for which, b in ((0, bg), (1, bu)):
                            for kc in range(NC8):
                                P.op("pe", lambda e, which=which, b=b, kc=kc, s=s, sub=sub: e.matmul(
                                    PS[b][:], WG[s][:, which, kc, :], HB[:, kc, sub * T:(sub + 1) * T],
                                    start=(kc == 0), stop=(kc == NC8 - 1)),
                                    r=[("wg", s), ("hb", sub, kc)], w=[("ps", b)])
                        q = sgi % 2
                        sgi += 1
                        P.op("act", lambda e, q=q, bg=bg: e.activation(out=SG[q][:], in_=PS[bg][:], func=AF.Silu),
                             r=[("ps", bg)], w=[("sg", q)])
                        P.op("dve", lambda e, q=q, bu=bu, j=j, sub=sub: e.tensor_tensor(
                            out=AA[:, j, sub * T:(sub + 1) * T], in0=SG[q][:], in1=PS[bu][:], op=ALU.mult),
                            r=[("sg", q), ("ps", bu)], w=[("aa", j, sub)])
                    if j + NS < NJ:
                        load_g(j + NS)
                    if j == NJ - 4:
                        load_d(0)
                        load_d(1)
                        load_d(2)
                if st == 0:
                    pre_hb(tag, 2, GV[("pre_ffn", l)])
                    pre_hb(tag, 3, GV[("pre_ffn", l)])
                elif hook is not None:
                    drain(hook(0))
                    drain(hook(1))
                for sub in range(2):
                    i = 2 * st + sub
                    for co in range(NC8):
                        k = sub * NC8 + co
                        s = k % 3
                        b = ps_next()
                        for kc in range(NJ):
                            P.op("pe", lambda e, b=b, kc=kc, s=s, sub=sub: e.matmul(
                                PS[b][:], WD[s][:, kc, :], AA[:, kc, sub * T:(sub + 1) * T], start=(kc == 0), stop=(kc == NJ - 1)),
                                r=[("wd", s), ("aa", kc, sub)], w=[("ps", b)])
                        if k + 3 < 2 * NC8:
                            load_d(k + 3)
                        evac_y(b, co)
                    postnorm_residual(i, GV[("post_ffn", l)])

    def ple(l, hook=None):
        tag = ("ple", l)
        with Scope(P, ["wga", "wpr", "ptt", "gt", "ysb", "sq8", "sqp8"]) as sc:
            YSB = sc.sb("YSB", [128, NC8, T], F32, "ysb")
            SQ8 = sc.sb("SQ8", [128, NC8, T], BF16, "sq8")
            SQP8 = sc.sb("SQP8", [128, NC8, T], BF16, "sqp8")
            GT = [sc.sb("GT%d" % s_, [128, T], F32, "gt") for s_ in range(2)]
            PTT = sc.sb("PTT", [128, 2, S], BF16, "ptt")
            WGA = sc.sb("WGA", [128, NC8, D], BF16, "wga")
            WPR = sc.sb("WPR", [128, 2, D], BF16, "wpr")
            P.dma("pool", lambda: (WGA[:, :, 0:512], wgate[l, :, :, 0:512]), "wga0", w=[("wga", 0)])
            P.dma("pool", lambda: (WPR[:], wproj[l, :, :, :]), "wpr", w=[("wpr",)])
            P.dma("pool", lambda: (PTT[:], pT[l, :, :, :]), "ptt", w=[("ptt",)])
            P.dma("pool", lambda: (WGA[:, :, 512:1024], wgate[l, :, :, 512:1024]), "wga1", w=[("wga", 1)])

            saved_cfg = {k: list(v) for k, v in statcfg.items()}
            statcfg[False] = [SQ8, "sq8", 8, 8]
            statcfg[True] = [SQP8, "sqp8", 8, 8]

            def main_g(i, ys, yk):
                q = i % 2
                cs = slice(i * T, (i + 1) * T)
                for co in range(NC8):
                    ba = ps_next()
                    bb = ps_next()
                    for kc in range(NC8):
                        P.op("pe", lambda e, kc=kc, co=co, ba=ba: e.matmul(
                            PS[ba][:], WGA[:, kc, co * 128:(co + 1) * 128], HB[:, kc, q * T:(q + 1) * T], start=(kc == 0), stop=(kc == NC8 - 1)),
                            r=[("wga", co // 4), ("hb", q, kc)], w=[("ps", ba)])
                    for kc in range(2):
                        P.op("pe", lambda e, kc=kc, co=co, bb=bb: e.matmul(
                            PS[bb][:], WPR[:, kc, co * 128:(co + 1) * 128], PTT[:, kc, cs], start=(kc == 0), stop=(kc == 1)),
                            r=[("wpr",), ("ptt",)], w=[("ps", bb)])
                    g2 = co % 2
                    P.op("act", lambda e, g2=g2, ba=ba: e.activation(out=GT[g2][:], in_=PS[ba][:], func=AF.Sigmoid),
                         r=[("ps", ba)], w=[("gt", g2)])
                    P.op("dve", lambda e, g2=g2, bb=bb, co=co: e.tensor_tensor(out=ys[:, co, :], in0=GT[g2][:], in1=PS[bb][:], op=ALU.mult),
                         r=[("gt", g2), ("ps", bb)], w=[(yk, co)])
                    stat_add(co, lambda e, o, co=co: e.activation(out=o, in_=ys[:, co, :], func=AF.Square), [(yk, co)])
                    yield

            run_tiles(lambda i: pre_hb_g(tag, i, GV[("ple_g", l)]), main_g, GV[("ple_post", l)], hook, YSB)
            statcfg[False] = saved_cfg[False]
            statcfg[True] = saved_cfg[True]

    def kvproj(KT, VP, hook=None):
        with Scope(P, ["wkk", "wvv", "sqp8"]) as sc:
            sc.sb("SPC", [128, 15360], BF16)
            WKK = sc.sb("WKK", [128, NC8, 4, 128], BF16, "wkk")
            WVV = sc.sb("WVV", [128, NC8, 256], BF16, "wvv")
            SQP8 = sc.sb("SQP8k", [128, NC8, T], BF16, "sqp8")
            P.dma("pool", lambda: (WKK[:], wk[:, :, :, :]), "wkk", w=[("wkk",)])
            P.dma("pool", lambda: (WVV[:], wv[:, :, :]), "wvv", w=[("wvv",)])
            P.op("pool", lambda e: e.memset(VP[:], 0.0), w=[("vp", n) for n in range(16)])
            saved = list(statcfg[True])
            statcfg[True] = [SQP8, "sqp8", 8, 8]

            def main_g(i):
                q = i % 2
                cs = slice(i * T, (i + 1) * T)
                for kvh in range(4):
                    b = ps_next()
                    for kc in range(NC8):
                        P.op("pe", lambda e, kc=kc, kvh=kvh, b=b: e.matmul(
                            PS[b][:], WKK[:, kc, kvh, :], HB[:, kc, q * T:(q + 1) * T], start=(kc == 0), stop=(kc == NC8 - 1)),
                            r=[("wkk",), ("hb", q, kc)], w=[("ps", b)])
                    P.op("act", lambda e, kvh=kvh, b=b: e.activation(out=KT[:, kvh, cs], in_=PS[b][:], func=AF.Copy),
                         r=[("ps", b)], w=[("kt", i)])
                    yield
                for tb in range(4):
                    n = 4 * i + tb
                    b = ps_next()
                    for kc in range(NC8):
                        P.op("pe", lambda e, kc=kc, tb=tb, b=b: e.matmul(
                            PS[b][:, 0:256], HB[:, kc, q * T + tb * 128:q * T + (tb + 1) * 128], WVV[:, kc, :], start=(kc == 0), stop=(kc == NC8 - 1)),
                            r=[("wvv",), ("hb", q, kc)], w=[("ps", b)])
                    P.op("dve", lambda e, n=n, b=b: e.tensor_copy(
                        out=VP[:, n, 64:576].rearrange("p (k d) -> p k d", k=4)[:, :, 0:64], in_=PS[b][:, 0:256].rearrange("p (k d) -> p k d", k=4)),
                        r=[("ps", b)], w=[("vp", n)])
                    yield

            drain(pre_hb_g("kv", 0, GV["kv_g"]))
            for i in range(NT):
                if i + 1 < NT:
                    nxt = pre_hb_g("kv", i + 1, GV["kv_g"])
                else:
                    nxt = hook(0) if hook is not None else None
                bgs = [nxt] if nxt is not None else []
                for _ in main_g(i):
                    run_bg(bgs, 2)
                drain(*bgs)
            if hook is not None:
                drain(hook(1))
            statcfg[True] = saved

    def mixer_attn(KT, VP, hook=None):
        slopes = [2.0 ** (-8.0 * (h + 1) / 16.0) for h in range(16)]
        NSL = 12
        with Scope(P, ["wqq", "woo", "qt", "at", "uu", "pt", "ll"]) as sc:
            QT = [sc.sb("QT0", [128, NC8, T], BF16, "qt")] * 2
            UU = [sc.sb("UU%d" % s_, [128, 256], F32, "uu") for s_ in range(4)]
            PT = [sc.sb("PT%d" % s_, [128, 256], BF16, "pt") for s_ in range(NSL)]
            LL = [sc.sb("LL%d" % s_, [128, 2, 128], F32, "ll") for s_ in range(2)]
            WQQ = sc.sb("WQQ", [128, NC8, D], BF16, "wqq")
            AT = sc.sb("AT", [128, NC8, T], BF16, "at")
            WOO = sc.sb("WOO", [128, NC8, D], BF16, "woo")
            P.dma("pool", lambda: (WQQ[:, 0:4, :], wq[:, 0:4, :]), "wqq", w=[("wqq",)])
            P.dma("pool", lambda: (WQQ[:, 4:8, :], wq[:, 4:8, :]), "wqq", w=[("wqq",)])
            P.dma("pool", lambda: (WOO[:, 0:4, :], wo[:, 0:4, :]), "woo", w=[("woo",)])
            P.dma("pool", lambda: (WOO[:, 4:8, :], wo[:, 4:8, :]), "woo", w=[("woo",)])
            gcnt = [0]

            def pre_a(i):
                pre_hb("attn", i, GV[("pre_mix", 1)])

            def pre_b_g(i, banks=None):
                q = 0
                hq = i % 2
                for co in range(NC8):
                    b = ps_next() if banks is None else banks[co % len(banks)]
                    for kc in range(NC8):
                        P.op("pe", lambda e, kc=kc, co=co, b=b: e.matmul(
                            PS[b][:], WQQ[:, kc, co * 128:(co + 1) * 128], HB[:, kc, hq * T:(hq + 1) * T], start=(kc == 0), stop=(kc == NC8 - 1)),
                            r=[("wqq",), ("hb", hq, kc)], w=[("ps", b)])
                    P.op("act", lambda e, co=co, b=b: e.activation(out=QT[q][:, co, :], in_=PS[b][:], func=AF.Copy),
                         r=[("ps", b)], w=[("qt", q, co)])
                    yield

            def main(i):
                qq = 0
                items = [(nq, k) for nq in range(4) for k in range(4)]
                base = gcnt[0]
                gcnt[0] += len(items)

                def geom(t):
                    nq, k = items[t]
                    n = 4 * i + nq
                    kbs = [n - 1, n] if n > 0 else [n]
                    g = base + t
                    return nq, k, n, kbs, 128 * len(kbs), g % 2, (g % 3) * 4

                def emit_S(t):
                    nq, k, n, kbs, width, par, s0 = geom(t)
                    for e4 in range(4):
                        j = 2 * k + e4 // 2
                        hh = e4 % 2
                        b = (0, 2)[par] + hh
                        off = (e4 // 2) * 256
                        pr = slice(hh * 64, (hh + 1) * 64)
                        for kbi, kb in enumerate(kbs):
                            P.op("pe", lambda e, b=b, pr=pr, k=k, kb=kb, kbi=kbi, j=j, nq=nq, off=off: e.matmul(
                                PS[b][:, off + kbi * 128:off + (kbi + 1) * 128], KT[pr, k, kb * 128:(kb + 1) * 128],
                                QT[qq][pr, j, nq * 128:(nq + 1) * 128], start=True, stop=True),
                                r=[("kt", kb // 4), ("qt", qq, j)], w=[("ps", b)])

                def emit_soft(t):
                    nq, k, n, kbs, width, par, s0 = geom(t)
                    mlo = 0 if len(kbs) == 2 else 128
                    for e4 in range(4):
                        h = 4 * k + e4
                        hh = e4 % 2
                        b = (0, 2)[par] + hh
                        off = (e4 // 2) * 256
                        sl = s0 + e4
                        ul = e4
                        P.op("dve", lambda e, b=b, sl=ul, h=h, mlo=mlo, width=width, off=off: e.scalar_tensor_tensor(
                            out=UU[sl][:, 0:width], in0=MT[:, mlo:mlo + width], scalar=slopes[h] / SCALE, in1=PS[b][:, off:off + width],
                            op0=ALU.mult, op1=ALU.add), r=[("mt",), ("ps", b)], w=[("uu", ul)])
                        P.op("act", lambda e, sl=sl, ul=ul, h=h, width=width: e.activation(
                            out=PT[sl][:, 0:width], in_=UU[ul][:, 0:width], func=AF.Exp, bias=NSINK[:, h:h + 1], scale=SCALE),
                            r=[("uu", ul), ("nsink",)], w=[("pt", sl)])

                def emit_PV(t):
                    nq, k, n, kbs, width, par, s0 = geom(t)
                    bo = 4 + par
                    nmm = 2 * len(kbs)
                    for pp in range(2):
                        for which in range(2):
                            m = 0
                            c0 = pp * 256 + which * 128
                            for hh in range(2):
                                sl = s0 + pp * 2 + hh
                                for kbi, kb in enumerate(kbs):
                                    if which == 0:
                                        lf = (lambda kb=kb, k=k, hh=hh: VP[:, kb, 64 + 128 * k:192 + 128 * k] if hh == 0 else VP[:, kb, 128 * k:128 * k + 128])
                                        rk = [("vp", kb)]
                                    else:
                                        lf = (lambda hh=hh: OP[:, 64:192] if hh == 0 else OP[:, 0:128])
                                        rk = [("op",)]
                                    P.op("pe", lambda e, lf=lf, bo=bo, c0=c0, sl=sl, kbi=kbi, m=m: e.matmul(
                                        PS[bo][:, c0:c0 + 128], lf(), PT[sl][:, kbi * 128:(kbi + 1) * 128],
                                        start=(m == 0), stop=(m == nmm - 1)),
                                        r=rk + [("pt", sl)], w=[("ps", bo)])
                                    m += 1

                def emit_norm_act(t):
                    nq, k, n, kbs, width, par, s0 = geom(t)
                    bo = 4 + par
                    ov = lambda: PS[bo][:, :].rearrange("p (a b c) -> p a b c", a=2, b=2)
                    P.op("act", lambda e: e.activation(out=LL[par][:], in_=ov()[:, :, 1, :], func=AF.Ln, bias=EPST[:, 1:2], scale=1.0),
                         r=[("ps", bo), ("eps",)], w=[("ll", par)])
                    P.op("act", lambda e: e.activation(out=LL[par][:], in_=LL[par][:], func=AF.Exp, scale=-1.0),
                         r=[("ll", par)], w=[("ll", par)])

                def emit_norm_dve(t):
                    nq, k, n, kbs, width, par, s0 = geom(t)
                    bo = 4 + par
                    ov = lambda: PS[bo][:, :].rearrange("p (a b c) -> p a b c", a=2, b=2)
                    P.op("dve", lambda e: e.tensor_tensor(
                        out=AT[:, 2 * k:2 * k + 2, nq * 128:(nq + 1) * 128], in0=ov()[:, :, 0, :], in1=LL[par][:], op=ALU.mult),
                        r=[("ps", bo), ("ll", par)], w=[("at", 2 * k), ("at", 2 * k + 1)])

                ni = len(items)
                for st_ in range(ni + 3):
                    if 0 <= st_ - 3 < ni:
                        emit_norm_act(st_ - 3)
                    if st_ < ni:
                        emit_S(st_)
                    if 0 <= st_ - 1 < ni:
                        emit_soft(st_ - 1)
                    if 0 <= st_ - 2 < ni:
                        emit_PV(st_ - 2)
                    if 0 <= st_ - 3 < ni:
                        emit_norm_dve(st_ - 3)
                    yield st_

            def oproj(i):
                for co in range(NC8):
                    b = ps_next()
                    for kc in range(NC8):
                        P.op("pe", lambda e, kc=kc, co=co, b=b: e.matmul(
                            PS[b][:], WOO[:, kc, co * 128:(co + 1) * 128], AT[:, kc, :], start=(kc == 0), stop=(kc == NC8 - 1)),
                            r=[("woo",), ("at", kc)], w=[("ps", b)])
                    evac_y(b, co)

            pre_a(0)
            pre_a(1)
            drain(pre_b_g(0))
            pend = None
            for i in range(NT):
                qg = pre_b_g(i + 1, banks=[0, 1, 2, 3]) if i + 1 < NT else None
                for st_ in main(i):
                    if pend is not None:
                        run_bg([pend], 1)
                    if qg is not None and st_ >= 16:
                        run_bg([qg], 3)
                drain(pend, qg)
                if i + 2 < NT:
                    pre_a(i + 2)
                elif hook is not None:
                    drain(hook(i + 2 - NT))
                oproj(i)
                pend = postnorm_begin(i, GV[("post_mix", 1)], fine=(i + 1 < NT))
            drain(pend)

    full = stop_after is None
    mixer_pool(hook=lambda k: pre_hb_g(("ffn", 0), k, GV[("pre_ffn", 0)]))
    if stop_after == "pool":
        return store_out()
    ffn(0, hook=lambda k: pre_hb_g(("ple", 0), k, GV[("ple_g", 0)]))
    if stop_after == "ffn0":
        return store_out()
    ple(0, hook=(lambda k: pre_hb_g("kv", k, GV["kv_g"])) if full else None)
    if stop_after == "ple0":
        return store_out()
    with Scope(P, ["kt", "vp"]) as kvs:
        KT = kvs.sb("KT", [128, 4, S], BF16, "kt")
        VP = kvs.sb("VP", [128, 16, 576], BF16, "vp")
        kvproj(KT, VP, hook=lambda k: pre_hb_g("attn", k, GV[("pre_mix", 1)]))
        mixer_attn(KT, VP, hook=(lambda k: pre_hb_g(("ffn", 1), k, GV[("pre_ffn", 1)])) if full else None)
    if stop_after == "attn":
        return store_out()
    ffn(1, hook=lambda k: pre_hb_g(("ple", 1), k, GV[("ple_g", 1)]))
    ple(1)
    return store_out()


_CACHE = {}


def get_program(stop_after=None):
    if stop_after not in _CACHE:
        _, Pd = build(True, None, stop_after)
        nc, Pr = build(False, Pd.need, stop_after)
        _CACHE[stop_after] = nc
    return _CACHE[stop_after]


def prep_shared(inp):
    f = lambda a: np.ascontiguousarray(a, dtype=np.float32)
    sh = {}
    vecs = []
    for l in range(2):
        for n in ["pre_mix_g", "post_mix_g", "pre_ffn_g", "post_ffn_g", "ple_g", "ple_post_g"]:
            vecs.append(inp[n][l])
    vecs.append(inp["pool_scale"][0])
    vecs.append(inp["kv_g"])
    g = np.stack([np.asarray(v, np.float32) for v in vecs], 0)
    sh["gains"] = f(g.reshape(NV, NC8, 128).transpose(2, 0, 1))
    sh["sinkb"] = f(np.broadcast_to(np.asarray(inp["sinks"], np.float32).reshape(1, 16), (128, 16)))
    ki = np.arange(128)[:, None]
    qr = np.arange(256)[None, :]
    rel = qr - ki
    m = np.where((rel >= 0) & (rel < 128), -rel.astype(np.float32), -1.0e6)
    sh["maskT"] = f(np.concatenate([m[:, 128:256], m[:, 0:128]], axis=1))
    ic = np.zeros((128, 4, 2, 16), np.float32)
    for gi, w in enumerate((2, 4, 8, 16)):
        ic[:, gi, :, :] = 1.0 / np.minimum(np.arange(1, 17), w).astype(np.float32)
    sh["invc"] = ic
    pw = np.asarray(inp["pool_w"], np.float32)[0]
    sh["wpool"] = f(pw.reshape(4, 2, 128, 256).transpose(2, 0, 1, 3))
    wgu = np.asarray(inp["w_gu"], np.float32)
    sh["wgu"] = f(wgu.reshape(2, NC8, 128, 2, NJ, 128).transpose(0, 4, 2, 3, 1, 5))
    wd = np.asarray(inp["w_down"], np.float32)
    sh["wdn"] = f(wd.reshape(2, NJ, 128, NC8, 128).transpose(0, 3, 2, 1, 4))
    wg = np.asarray(inp["w_ple_gate"], np.float32)
    sh["wgate"] = f(wg.reshape(2, NC8, 128, D).transpose(0, 2, 1, 3))
    wp = np.asarray(inp["w_ple_proj"], np.float32)
    sh["wproj"] = f(wp.reshape(2, 2, 128, D).transpose(0, 2, 1, 3))
    sh["wq"] = f(np.asarray(inp["w_q"], np.float32)[0].reshape(NC8, 128, D).transpose(1, 0, 2))
    sh["wo"] = f(np.asarray(inp["w_o"], np.float32)[0].reshape(NC8, 128, D).transpose(1, 0, 2))
    wkv = np.asarray(inp["w_kv"], np.float32)
    k4 = wkv[:, :256].reshape(NC8, 128, 4, 64).transpose(1, 0, 2, 3)
    sh["wk"] = f(np.concatenate([k4, k4], axis=3))
    sh["wv"] = f(wkv[:, 256:].reshape(NC8, 128, 256).transpose(1, 0, 2))
    return sh


def kernel(**inp):
    stop_after = inp.pop("_stop_after", None)
    nc = get_program(stop_after)
    sh = prep_shared(inp)
    x = np.asarray(inp["x"], np.float32)
    p = np.asarray(inp["p"], np.float32)
    in_maps = []
    for b in range(8):
        m = dict(sh)
        m["xT"] = np.ascontiguousarray(x[b].T.reshape(NC8, 128, S).transpose(1, 0, 2))
        m["pT"] = np.ascontiguousarray(p[:, b].transpose(0, 2, 1).reshape(2, 2, 128, S).transpose(0, 2, 1, 3))
        in_maps.append(m)
    res = run_bass_kernel_spmd(nc, in_maps, core_ids=list(range(8)))
    out = np.empty((8, S, D), np.float32)
    for b in range(8):
        o = res.results[b]["outT"]
        out[b] = o.transpose(1, 0, 2).reshape(D, S).T
    return out
```
